# Optimizing a Trainium2 kernel written in Bass

```python
import jax, jax.numpy as jnp
from jax import lax
import numpy as np

D_MODEL = 1024
BATCH = 16
SEQ = 2048
DEPTH = 2

CHUNK = 64
N_MIXERS = 2
EXPAND = 2
D_INNER = EXPAND * D_MODEL
SG_BLOCK = 128
SG_GROUPS = 8
SG_GROUP_DIM = D_INNER // SG_GROUPS
HG_HEADS = 16
HG_HEAD_DIM = D_INNER // HG_HEADS
N_LAYERS_A = (DEPTH + 1) // 2
N_LAYERS_B = DEPTH // 2
EPS = 1e-6

kernel_name = "hybrid_gmlp_hgrn2_adaln_trunk"


def rms_norm(x, gain):
    xf = x.astype(jnp.float32)
    y = xf * lax.rsqrt(jnp.mean(xf * xf, axis=-1, keepdims=True) + EPS)
    return (y * gain.astype(jnp.float32)).astype(x.dtype)


def layer_norm(x, gain, bias):
    xf = x.astype(jnp.float32)
    mu = jnp.mean(xf, axis=-1, keepdims=True)
    var = jnp.mean(jnp.square(xf - mu), axis=-1, keepdims=True)
    y = (xf - mu) * lax.rsqrt(var + EPS) * gain.astype(jnp.float32) + bias.astype(jnp.float32)
    return y.astype(x.dtype)


def spatial_gating_mixer(h, w_in, ln_gain, ln_bias, w_s, b_s, w_out):
    bsz, seq, _ = h.shape
    proj = h @ w_in
    uv, g = proj[..., : 2 * D_INNER], proj[..., 2 * D_INNER:]
    uv = jax.nn.gelu(uv)
    u, v = uv[..., :D_INNER], uv[..., D_INNER:]
    v = layer_norm(v, ln_gain, ln_bias)
    nb = seq // SG_BLOCK
    v = v.reshape(bsz, nb, SG_BLOCK, SG_GROUPS, SG_GROUP_DIM)
    pos = jnp.arange(SG_BLOCK)
    mask = (pos[None, :] // CHUNK) <= (pos[:, None] // CHUNK)
    ws = jnp.where(mask[None], w_s, jnp.zeros((), w_s.dtype))
    s = jnp.einsum('gts,bnsgd->bntgd', ws, v) + b_s.T[None, None, :, :, None]
    s = s.reshape(bsz, seq, D_INNER)
    y = u * s * jax.nn.silu(g)
    return y @ w_out


def hgrn2_mixer(h, w_in, lower_bound, gn_gain, w_out):
    bsz, seq, _ = h.shape
    f32 = jnp.float32
    proj = h @ w_in
    q = proj[..., :D_INNER]
    f = proj[..., D_INNER: 2 * D_INNER]
    i = proj[..., 2 * D_INNER: 3 * D_INNER]
    g = proj[..., 3 * D_INNER:]
    q = jax.nn.silu(q.astype(f32))
    lb = lower_bound.astype(f32)
    f = lb + (1.0 - lb) * jax.nn.sigmoid(f.astype(f32))
    k = 1.0 - f
    log_f = jnp.log(f)
    nc = seq // CHUNK

    def heads(z):
        return z.reshape(bsz, nc, CHUNK, HG_HEADS, HG_HEAD_DIM).transpose(0, 3, 1, 2, 4)

    q, k, v, log_f = heads(q), heads(k), heads(i.astype(f32)), heads(log_f)
    a = jnp.cumsum(log_f, axis=3)
    a_ref = a[:, :, :, CHUNK // 2 - 1: CHUNK // 2, :]
    a_last = a[:, :, :, CHUNK - 1:, :]
    q_in = q * jnp.exp(a - a_ref)
    k_in = k * jnp.exp(a_ref - a)
    scores = jnp.einsum('bhnck,bhnsk->bhncs', q_in, k_in)
    causal = jnp.tril(jnp.ones((CHUNK, CHUNK), dtype=bool))
    scores = jnp.where(causal, scores, jnp.zeros((), f32))
    o_intra = jnp.einsum('bhncs,bhnsv->bhncv', scores, v)
    q_out = q * jnp.exp(a)
    k_out = k * jnp.exp(a_last - a)
    decay = jnp.exp(a_last[:, :, :, 0, :])

    def step(state, xs):
        q_c, k_c, v_c, d_c = xs
        o_c = jnp.einsum('bhck,bhkv->bhcv', q_c, state)
        state = d_c[..., None] * state + jnp.einsum('bhck,bhcv->bhkv', k_c, v_c)
        return state, o_c

    xs = (jnp.moveaxis(q_out, 2, 0), jnp.moveaxis(k_out, 2, 0),
          jnp.moveaxis(v, 2, 0), jnp.moveaxis(decay, 2, 0))
    init = jnp.zeros((bsz, HG_HEADS, HG_HEAD_DIM, HG_HEAD_DIM), f32)
    _, o_inter = lax.scan(step, init, xs)
    o = o_intra + jnp.moveaxis(o_inter, 0, 2)
    o = o.transpose(0, 2, 3, 1, 4)
    o = rms_norm(o, gn_gain)
    o = o.reshape(bsz, seq, D_INNER).astype(h.dtype)
    return (o * jax.nn.silu(g)) @ w_out


def setup_inputs(seed: int = 0) -> dict:
    key = jax.random.key(seed)
    ks = jax.random.split(key, 20)
    nrm = jax.random.normal
    f32 = jnp.float32
    D, DI = D_MODEL, D_INNER
    return {
        "x": nrm(ks[0], (BATCH, SEQ, D), f32),
        "c": nrm(ks[1], (BATCH, D), f32),
        "norm_gain": 1.0 + 0.02 * nrm(ks[2], (DEPTH, D), f32),
        "w_ada": 0.5 * D ** -0.5 * nrm(ks[3], (DEPTH, D, 3 * D), f32),
        "b_ada": 0.02 * nrm(ks[4], (DEPTH, 3 * D), f32),
        "a_w_in": D ** -0.5 * nrm(ks[5], (N_LAYERS_A, D, 3 * DI), f32),
        "a_ln_gain": 1.0 + 0.02 * nrm(ks[6], (N_LAYERS_A, DI), f32),
        "a_ln_bias": 0.02 * nrm(ks[7], (N_LAYERS_A, DI), f32),
        "a_w_s": SG_BLOCK ** -0.5 * nrm(ks[8], (N_LAYERS_A, SG_GROUPS, SG_BLOCK, SG_BLOCK), f32),
        "a_b_s": 1.0 + 0.02 * nrm(ks[9], (N_LAYERS_A, SG_GROUPS, SG_BLOCK), f32),
        "a_w_out": DI ** -0.5 * nrm(ks[10], (N_LAYERS_A, DI, D), f32),
        "b_w_in": D ** -0.5 * nrm(ks[11], (N_LAYERS_B, D, 4 * DI), f32),
        "b_lower_bounds": 0.1 * nrm(ks[12], (DEPTH, DI), f32),
        "b_gn_gain": 1.0 + 0.02 * nrm(ks[13], (N_LAYERS_B, HG_HEAD_DIM), f32),
        "b_w_out": DI ** -0.5 * nrm(ks[14], (N_LAYERS_B, DI, D), f32),
        "final_gain": 1.0 + 0.02 * nrm(ks[15], (D,), f32),
    }


def reference(x, c, norm_gain, w_ada, b_ada, a_w_in, a_ln_gain, a_ln_bias, a_w_s, a_b_s,
              a_w_out, b_w_in, b_lower_bounds, b_gn_gain, b_w_out, final_gain):
    p = jax.nn.softmax(b_lower_bounds.astype(jnp.float32), axis=0)
    cum = jnp.cumsum(p, axis=0)
    lower_bounds = cum - cum[0:1]
    c_act = jax.nn.silu(c)
    for layer in range(DEPTH):
        mod = c_act @ w_ada[layer] + b_ada[layer]
        shift = mod[:, None, :D_MODEL]
        scale = mod[:, None, D_MODEL: 2 * D_MODEL]
        gate = mod[:, None, 2 * D_MODEL:]
        h = rms_norm(x, norm_gain[layer]) * (1.0 + scale) + shift
        j = layer // N_MIXERS
        if layer % N_MIXERS == 0:
            y = spatial_gating_mixer(h, a_w_in[j], a_ln_gain[j], a_ln_bias[j],
                                     a_w_s[j], a_b_s[j], a_w_out[j])
        else:
            y = hgrn2_mixer(h, b_w_in[j], lower_bounds[layer], b_gn_gain[j], b_w_out[j])
        x = x + gate * y
    return rms_norm(x, final_gain)
```

```python
import math
from contextlib import ExitStack

import numpy as np
import concourse.bass as bass
import concourse.mybir as mybir
from concourse.bass_utils import run_bass_kernel_spmd

F32 = mybir.dt.float32
BF16 = mybir.dt.bfloat16
AF = mybir.ActivationFunctionType
ALU = mybir.AluOpType

ENGS = ("sync", "scalar", "vector", "gpsimd", "tensor")

D = 1024
DI = 2048
KC = 8
NT = 512
NST = 4
EPS = 1e-6
N_CORES = 8
DEBUG = False
DUMPNAMES = []
LAST = []


class Buf:
    __slots__ = ("name", "last_w", "readers")

    def __init__(self, name):
        self.name = name
        self.last_w = None
        self.readers = []


class Op:
    __slots__ = ("eng", "fn", "idx", "deps", "dma_sem", "signal", "sigcount", "waits", "known", "gidx")

    def __init__(self, eng, fn, dma_sem):
        self.eng = eng
        self.fn = fn
        self.dma_sem = dma_sem
        self.deps = []
        self.signal = False
        self.sigcount = 0
        self.waits = []
        self.known = None


class Prog:
    def __init__(self, nc):
        self.nc = nc
        self.ops = []
        self.per_eng = {e: [] for e in ENGS}

    def add(self, eng, fn, reads=(), writes=(), dma_sem=None):
        op = Op(eng, fn, dma_sem)
        op.gidx = len(self.ops)
        op.idx = len(self.per_eng[eng])
        deps = {}
        for b in reads:
            if b.last_w is not None:
                deps[id(b.last_w)] = b.last_w
        for b in writes:
            if b.last_w is not None:
                deps[id(b.last_w)] = b.last_w
            for r in b.readers:
                deps[id(r)] = r
        op.deps = list(deps.values())
        for b in reads:
            b.readers.append(op)
        for b in writes:
            b.last_w = op
            b.readers = []
        self.ops.append(op)
        self.per_eng[eng].append(op)
        return op

    def finalize(self, stack):
        nc = self.nc
        dma_cum = {}
        for op in self.ops:
            if op.dma_sem is not None:
                dma_cum[op.dma_sem] = dma_cum.get(op.dma_sem, 0) + 16
                op.sigcount = dma_cum[op.dma_sem]
        eng_known = {e: {} for e in ENGS}
        for op in self.ops:
            kn = eng_known[op.eng]
            waits = []
            for p in sorted(op.deps, key=lambda q: -q.gidx):
                if p.dma_sem is not None:
                    key = ("d", p.dma_sem)
                    val = p.sigcount
                else:
                    key = p.eng
                    val = p.idx + 1
                    if p.eng == op.eng:
                        if op.eng == "tensor" or op.eng == "sync":
                            continue
                if kn.get(key, 0) >= val:
                    continue
                waits.append(p)
                kn[key] = val
                if p.known is not None:
                    for k2, v2 in p.known.items():
                        if kn.get(k2, 0) < v2:
                            kn[k2] = v2
            op.waits = waits
            for p in waits:
                p.signal = True
            kn2 = dict(kn)
            if op.dma_sem is None:
                kn2[op.eng] = op.idx + 1
            op.known = kn2
        for e in ENGS:
            c = 0
            for op in self.per_eng[e]:
                if op.dma_sem is None and op.signal:
                    c += 1
                    op.sigcount = c
        sems = {}
        for e in ENGS:
            sems[e] = stack.enter_context(nc.semaphore("sem_" + e))
        for k in dma_cum:
            sems[("d", k)] = stack.enter_context(nc.semaphore("semd_" + str(k)))
        block = stack.enter_context(nc.Block())

        def emit_engine(eng_name):
            def body(eng):
                for op in self.per_eng[eng_name]:
                    for p in op.waits:
                        if p.dma_sem is not None:
                            eng.wait_ge(sems[("d", p.dma_sem)], p.sigcount)
                        else:
                            eng.wait_ge(sems[p.eng], p.sigcount)
                    if op.fn is None:
                        continue
                    inst = op.fn(eng)
                    if op.dma_sem is not None:
                        inst.then_inc(sems[("d", op.dma_sem)], 16)
                    elif op.signal:
                        inst.then_inc(sems[eng_name], 1)
            return body

        for e in ENGS:
            if self.per_eng[e]:
                getattr(block, e)(emit_engine(e))


def build(NSEQ, SEQ):
    NSUP = SEQ // NT
    NTOK = NSEQ * SEQ
    nc = bass.Bass("TRN2", target_bir_lowering=False)

    def din(name, shape):
        return nc.dram_tensor(name, list(shape), F32, kind="ExternalInput").ap()

    x_d = din("x", [NTOK, D])
    c_d = din("c", [NSEQ, D])
    ng_d = din("norm_gain", [2, D])
    wada_d = din("w_ada", [2, D, 3 * D])
    bada_d = din("b_ada", [2, 3 * D])
    awin_d = din("a_w_in", [1, D, 3 * DI])
    alng_d = din("a_ln_gain", [1, DI])
    alnb_d = din("a_ln_bias", [1, DI])
    aws_d = din("a_w_s", [1, 8, 128, 128])
    abs_d = din("a_b_s", [1, 8, 128])
    awout_d = din("a_w_out", [1, DI, D])
    bwin_d = din("b_w_in", [1, D, 4 * DI])
    blb_d = din("b_lower_bounds", [2, DI])
    bgn_d = din("b_gn_gain", [1, 128])
    bwout_d = din("b_w_out", [1, DI, D])
    fg_d = din("final_gain", [D])
    out_d = nc.dram_tensor("out", [NTOK, D], F32, kind="ExternalOutput").ap()
    dbg_d = nc.dram_tensor("dbg", [NTOK, D], F32, kind="ExternalOutput").ap() if DEBUG else None

    blocks = []

    def defblk(halves):
        blocks.append(halves)
        return len(blocks) - 1

    AW = awin_d[0]
    BW = bwin_d[0]
    L0_V = [defblk([(AW, 0, DI + vb * 512), (AW, 0, DI + vb * 512 + 256)]) for vb in range(4)]
    L0_U = []
    L0_G = []
    for fg in range(4):
        L0_U.append(defblk([(AW, 0, fg * 512), (AW, 0, fg * 512 + 256)]))
        L0_G.append(defblk([(AW, 0, 2 * DI + fg * 512), (AW, 0, 2 * DI + fg * 512 + 256)]))
    L0_O = {}
    for ch in range(2):
        for kh in range(2):
            L0_O[(kh, ch)] = defblk([(awout_d[0], kh * 1024, ch * 512), (awout_d[0], kh * 1024, ch * 512 + 256)])
    L1_A = []
    L1_B = []
    for hp in range(8):
        L1_A.append(defblk([(BW, 0, hp * 256), (BW, 0, DI + hp * 256)]))
        L1_B.append(defblk([(BW, 0, 3 * DI + hp * 256), (BW, 0, 2 * DI + hp * 256)]))
    L1_O = {}
    for ch in range(2):
        for kh in range(2):
            L1_O[(kh, ch)] = defblk([(bwout_d[0], kh * 1024, ch * 512), (bwout_d[0], kh * 1024, ch * 512 + 256)])
    NBLK = len(blocks)
    wsc = nc.dram_tensor("wsc", [NBLK, 128, KC, 512], BF16, kind="Internal").ap()
    WSC = [Buf("wsc%d" % i) for i in range(NBLK)]

    with ExitStack() as st:
        st.enter_context(nc.allow_non_contiguous_dma(reason="small one-time parameter vectors"))

        def sb(name, shape, dt=F32):
            return st.enter_context(nc.sbuf_tensor(name, list(shape), dt))

        P = Prog(nc)
        rec = [None]

        def A(eng, fn, reads=(), writes=(), dma_sem=None):
            if rec[0] is not None:
                rec[0].append((eng, fn, list(reads), list(writes), dma_sem))
            else:
                P.add(eng, fn, reads, writes, dma_sem)

        def record(f, *args):
            old = rec[0]
            rec[0] = []
            ret = f(*args)
            lst = rec[0]
            rec[0] = old
            return lst, ret

        def emit(lst):
            for it in lst:
                A(*it)

        def merge(lists):
            lists = [l for l in lists if l]
            idx = [0] * len(lists)
            out = []
            while True:
                best = None
                for k, l in enumerate(lists):
                    if idx[k] < len(l):
                        frac = (idx[k] + 0.5) / len(l)
                        if best is None or frac < best[0]:
                            best = (frac, k)
                if best is None:
                    break
                k = best[1]
                out.append(lists[k][idx[k]])
                idx[k] += 1
            return out
        OUTB = [Buf("out0"), Buf("out1")]
        DBGB = Buf("dbg")

        NSLOT = 32
        tpool = sb("tpool", [128, NSLOT, 512])
        TB = [Buf("T%d" % i) for i in range(NSLOT)]
        xt = sb("xt", [128, NST, D])
        XB = [Buf("x%d" % i) for i in range(NST)]
        hT = sb("hT", [128, KC, NT], BF16)
        HB = [Buf("h%d" % i) for i in range(NST)]
        NW = 4
        wring = sb("wring", [128, NW, KC, 512], BF16)
        WBh = [[Buf("w%d_%d" % (i, h)) for h in range(2)] for i in range(NW)]
        shared16 = sb("shared16", [128, NST * DI], BF16)
        vhat = shared16[:, :].rearrange("p (a d) -> p a d", d=DI)
        VHB = [Buf("vh%d" % i) for i in range(NST)]
        yT = sb("yT", [128, 16, NT], BF16)
        YB = [Buf("y%d" % i) for i in range(16)]
        gate_bc = sb("gate_bc", [128, 2, D])
        GTB = [Buf("gt%d" % i) for i in range(2)]
        fgain_bc = sb("fgain_bc", [128, D])
        FGB = Buf("fgain")
        state = sb("state", [128, 16, 128])
        STB = [Buf("st%d" % h) for h in range(16)]
        ident = sb("ident", [128, 128])
        ident_bf = sb("ident_bf", [128, 128], BF16)
        ones_f = sb("ones_f", [128, 128])
        onesm_bf = sb("onesm_bf", [128, 128], BF16)
        mask01 = sb("mask01", [128, NT], BF16)
        cmask = sb("cmask", [128, 128])
        wsT_bf = sb("wsT_bf", [128, 8, 128], BF16)
        Ch = sb("Ch", [128, 16, 128])
        CONST = Buf("const")
        ngT = sb("ngT", [128, 2, KC])
        lng = sb("lng", [128, 16])
        lnb = sb("lnb", [128, 16])
        lngh = sb("lngh", [128, 16])
        lnbh = sb("lnbh", [128, 16])
        blb = sb("blb", [128, 2, 16])
        lbt = sb("lbt", [128, 4, 16])
        c1 = sb("c1", [128, 16])
        c2 = sb("c2", [128, 16])
        nc1 = sb("nc1", [128, 16])
        gng = sb("gng", [128, 1])
        gngh = sb("gngh", [128, 1])
        bs_bc = tpool[:, 4:6, :].rearrange("p a (g s) -> p (a g) s", s=128)
        bsh = bs_bc
        cT = sb("cT", [128, KC, NSEQ])
        cact = sb("cact", [128, KC, NSEQ])
        csig = sb("csig", [128, KC, NSEQ])
        bT = sb("bT", [128, 2, 24])
        rowt = sb("rowt", [NSEQ, 2, 256])
        ROWB = [Buf("rowt0"), Buf("rowt1")]
        modT = sb("modT", [128, 2, 24, NSEQ])
        gsT = sb("gsT", [128, 2, KC, NSEQ])
        mhalf = sb("mhalf", [128, 4])
        ss = sb("ss", [128, 3, NST])
        msq = sb("msq", [128, 3, NST])
        rstd = sb("rstd", [128, 3, NST])
        SSB = [[Buf("ss%d_%d" % (p_, i)) for i in range(NST)] for p_ in range(3)]
        bnst = sb("bnst", [128, 2, 4, 6])
        mv = sb("mv", [128, 2, 2])
        vtmp = sb("vtmp", [128, 2, 1])
        vrs = sb("vrs", [128, 2, 1])
        BNB = [Buf("bn%d" % i) for i in range(2)]
        q_inT = shared16[:, 0:2048].rearrange("p (s h t) -> p s h t", s=2, h=2)
        k_inT = shared16[:, 2048:4096].rearrange("p (s h t) -> p s h t", s=2, h=2)
        v_tok = shared16[:, 4096:7168].rearrange("p (s a c) -> p s a c", s=3, a=NST)
        QIB = [[Buf("qi%d_%d" % (p_, i)) for i in range(2)] for p_ in range(2)]
        KIB = [[Buf("ki%d_%d" % (p_, i)) for i in range(2)] for p_ in range(2)]
        VTB = [[Buf("vt%d_%d" % (p_, i)) for i in range(NST)] for p_ in range(3)]
        ALIASED = [b for r in QIB for b in r] + [b for r in KIB for b in r] + [b for r in VTB for b in r]
        fence_t = sb("fence_t", [128, 2])

        def fence():
            A("gpsimd", lambda e: e.memset(fence_t[:], 0.0), writes=VHB + ALIASED)
        k_tok = sb("k_tok", [128, 2, NST, 128], BF16)
        KTB = [Buf("ktok0"), Buf("ktok1")]
        scm = sb("scm", [128, 2, NST, 128], BF16)
        SCB = [Buf("scm%d" % i) for i in range(2)]
        sqb = sb("sqb", [128, 2, NT], BF16)
        SQB = [Buf("sq%d" % i) for i in range(2)]
        Sh = sb("Sh", [128, 2, 3, 128])
        SHB = [Buf("Sh%d" % i) for i in range(2)]
        stq = sb("stq", [128, 2, NST, 128], BF16)
        SQTB = [Buf("stq%d" % i) for i in range(2)]
        utmp = sb("utmp", [128, 2, NST, 128])
        UTB = [Buf("ut%d" % i) for i in range(2)]
        cols = sb("cols", [128, 2, 2, 5, NST])
        CLB = [[Buf("cl%d_%d" % (p_, i)) for i in range(2)] for p_ in range(2)]

        psum = [st.enter_context(nc.psum_tensor("ps%d" % i, [128, 512], F32)) for i in range(8)]
        PB = [Buf("ps%d" % i) for i in range(8)]
        ps_ctr = {"all": 0, "X": 0, "Y": 0}

        def nextps(pool="all"):
            if pool == "all":
                i = ps_ctr[pool] % 7
            elif pool == "X":
                i = ps_ctr[pool] % 4
            else:
                i = 4 + ps_ctr[pool] % 4
            ps_ctr[pool] += 1
            return psum[i], PB[i]

        def T(i):
            return tpool[:, i, :]

        def T2(i):
            return tpool[:, i:i + 2, :].rearrange("p a b -> p (a b)")

        def T4(i):
            return tpool[:, i:i + 4, :].rearrange("p a b -> p (a b)")

        dma_ctr = [0]

        def dkey(prefix="k"):
            dma_ctr[0] += 1
            return "%s%d" % (prefix, dma_ctr[0])

        dumps = {}

        def dump(name, ap, bufs, dt=F32):
            if not DEBUG or name in dumps:
                return
            shp = list(ap.shape)
            d = nc.dram_tensor("dbg_" + name, shp, dt, kind="ExternalOutput").ap()
            dumps[name] = d
            DUMPNAMES.append("dbg_" + name)
            A("sync", lambda e: e.dma_start(out=d, in_=ap), reads=bufs, writes=[DBGB], dma_sem="dbg_" + name)

        A("gpsimd", lambda e: e.memset(ident[:], 1.0), writes=[CONST])
        A("gpsimd", lambda e: e.affine_select(out=ident[:], in_=ident[:], pattern=[[-1, 128]],
                                              compare_op=ALU.is_equal, fill=0.0, base=0, channel_multiplier=1),
          reads=[CONST], writes=[CONST])
        A("gpsimd", lambda e: e.tensor_copy(out=ident_bf[:], in_=ident[:]), reads=[CONST], writes=[CONST])
        A("gpsimd", lambda e: e.memset(ones_f[:], 1.0), writes=[CONST])
        A("gpsimd", lambda e: e.memset(onesm_bf[:], 1.0 / 128.0), writes=[CONST])
        A("gpsimd", lambda e: e.memset(mask01[:], 1.0), writes=[CONST])
        A("gpsimd", lambda e: e.memset(mask01[:].rearrange("p (c t) -> p c t", t=128)[:, :, 0:1], 0.0),
          reads=[CONST], writes=[CONST])
        A("gpsimd", lambda e: e.memset(cmask[:], 1.0), writes=[CONST])
        A("gpsimd", lambda e: e.affine_select(out=cmask[:], in_=cmask[:], pattern=[[1, 128]],
                                              compare_op=ALU.is_ge, fill=0.0, base=0, channel_multiplier=-1),
          reads=[CONST], writes=[CONST])
        A("gpsimd", lambda e: e.memset(mhalf[:], -0.5), writes=[CONST])
        A("gpsimd", lambda e: e.memset(state[:], 0.0), writes=STB)

        def small_load(dst_ap, src_ap, bufs):
            k = dkey("c")
            A("sync", lambda e: e.dma_start(out=dst_ap, in_=src_ap), writes=bufs, dma_sem=k)

        VEC = Buf("vec")
        for l in range(2):
            small_load(ngT[:, l, :], ng_d[l].rearrange("(kc p) -> p kc", p=128), [VEC])
            small_load(blb[:, l, :], blb_d[l].rearrange("(fc p) -> p fc", p=128), [VEC])
        small_load(lng[:], alng_d[0].rearrange("(fc p) -> p fc", p=128), [VEC])
        small_load(lnb[:], alnb_d[0].rearrange("(fc p) -> p fc", p=128), [VEC])
        small_load(gng[:], bgn_d[0].rearrange("(p o) -> p o", o=1), [VEC])
        small_load(bs_bc, abs_d[0].partition_broadcast(128), [TB[4], TB[5]])
        small_load(fgain_bc[:], fg_d.partition_broadcast(128), [FGB])
        for s in range(NSEQ):
            small_load(cT[:, :, s], c_d[s].rearrange("(kc p) -> p kc", p=128), [VEC])
        for l in range(2):
            small_load(bT[:, l, :], bada_d[l].rearrange("(c p) -> p c", p=128), [VEC])

        A("vector", lambda e: e.tensor_scalar(out=lngh[:], in0=lng[:], scalar1=0.5, scalar2=None, op0=ALU.mult),
          reads=[VEC], writes=[CONST])
        A("vector", lambda e: e.tensor_scalar(out=lnbh[:], in0=lnb[:], scalar1=0.5, scalar2=None, op0=ALU.mult),
          reads=[VEC], writes=[CONST])
        A("vector", lambda e: e.tensor_scalar(out=gngh[:], in0=gng[:], scalar1=0.5, scalar2=None, op0=ALU.mult),
          reads=[VEC], writes=[CONST])
        A("vector", lambda e: e.tensor_scalar(out=bs_bc, in0=bs_bc, scalar1=0.5, scalar2=None, op0=ALU.mult),
          reads=[TB[4], TB[5]], writes=[TB[4], TB[5]])
        LBB = Buf("lb")
        A("vector", lambda e: e.tensor_tensor(out=lbt[:, 0, :], in0=blb[:, 0, :], in1=blb[:, 1, :], op=ALU.subtract),
          reads=[VEC], writes=[LBB])
        A("scalar", lambda e: e.activation(out=lbt[:, 1, :], in_=lbt[:, 0, :], func=AF.Exp), reads=[LBB], writes=[LBB])
        A("vector", lambda e: e.tensor_scalar(out=lbt[:, 2, :], in0=lbt[:, 1, :], scalar1=1.0, scalar2=None, op0=ALU.add),
          reads=[LBB], writes=[LBB])
        A("vector", lambda e: e.reciprocal(out=lbt[:, 3, :], in_=lbt[:, 2, :]), reads=[LBB], writes=[LBB])
        A("vector", lambda e: e.tensor_scalar(out=c1[:], in0=lbt[:, 3, :], scalar1=-0.5, scalar2=0.5, op0=ALU.mult, op1=ALU.add),
          reads=[LBB], writes=[CONST])
        A("vector", lambda e: e.tensor_scalar(out=c2[:], in0=lbt[:, 3, :], scalar1=0.5, scalar2=0.5, op0=ALU.mult, op1=ALU.add),
          reads=[LBB], writes=[CONST])
        A("vector", lambda e: e.tensor_scalar(out=nc1[:], in0=lbt[:, 3, :], scalar1=0.5, scalar2=-0.5, op0=ALU.mult, op1=ALU.add),
          reads=[LBB], writes=[CONST])
        A("scalar", lambda e: e.activation(out=csig[:], in_=cT[:], func=AF.Tanh, scale=0.5), reads=[VEC], writes=[LBB])
        A("vector", lambda e: e.tensor_scalar(out=csig[:], in0=csig[:], scalar1=0.5, scalar2=0.5, op0=ALU.mult, op1=ALU.add),
          reads=[LBB], writes=[LBB])
        A("vector", lambda e: e.tensor_tensor(out=cact[:], in0=csig[:], in1=cT[:], op=ALU.mult), reads=[LBB, VEC], writes=[CONST])

        wsl = tpool[:, 0:2, :].rearrange("p a (g s) -> p (a g) s", s=128)
        wsTf = tpool[:, 2:4, :].rearrange("p a (g s) -> p (a g) s", s=128)
        k = dkey("c")
        A("sync", lambda e: e.dma_start(out=wsl, in_=aws_d[0].rearrange("g t s -> t g s")), writes=[TB[0], TB[1]], dma_sem=k)
        for half in range(2):
            ps, pb = nextps()
            for j in range(4):
                g = half * 4 + j
                A("tensor", lambda e, ps=ps, g=g, j=j: e.transpose(out=ps[:, j * 128:(j + 1) * 128], in_=wsl[:, g, :], identity=ident[:]),
                  reads=[TB[0], TB[1], CONST], writes=[pb])
            A("vector", lambda e, ps=ps, half=half: e.tensor_copy(out=tpool[:, 2 + half, :], in_=ps[:]), reads=[pb], writes=[TB[2 + half]])
        A("gpsimd", lambda e: e.memset(wsTf[64:128, :, 0:64], 0.0), reads=[TB[2], TB[3]], writes=[TB[2], TB[3]])
        A("vector", lambda e: e.tensor_copy(out=wsT_bf[:], in_=wsTf), reads=[TB[2], TB[3]], writes=[CONST])
        for half in range(2):
            ps, pb = nextps()
            A("tensor", lambda e, ps=ps, half=half: e.matmul(ps[:], lhsT=ones_f[:], rhs=tpool[:, 2 + half, :], start=True, stop=True),
              reads=[TB[2 + half], CONST], writes=[pb])
            for j in range(4):
                g = half * 4 + j
                for fc in (2 * g, 2 * g + 1):
                    A("vector", lambda e, ps=ps, j=j, g=g, fc=fc: e.scalar_tensor_tensor(
                        out=Ch[:, fc, :], in0=ps[:, j * 128:(j + 1) * 128], scalar=lnbh[:, fc:fc + 1], in1=bsh[:, g, :],
                        op0=ALU.mult, op1=ALU.add), reads=[pb, CONST, VEC, TB[4], TB[5]], writes=[CONST])

        PF = [TB[4 * i:4 * i + 4] for i in range(3)]
        pf_ap = [tpool[:, 4 * i:4 * i + 4, :].rearrange("p a (k c) -> p (a k) c", c=256) for i in range(3)]
        pb16 = [vhat[:, i, :].rearrange("p (k c) -> p k c", c=256) for i in range(3)]
        stage_ctr = [0]

        def stage_load(W2d, r0, c0):
            i = stage_ctr[0] % 3
            stage_ctr[0] += 1
            src = W2d[r0:r0 + 1024, c0:c0 + 256].rearrange("(kc p) c -> p kc c", p=128)
            A("sync", lambda e: e.dma_start(out=pf_ap[i], in_=src), writes=PF[i], dma_sem="pf%d" % i)
            return i

        MODB = Buf("modr")
        for l in range(2):
            psT, pbT = psum[7], PB[7]
            for cb in range(12):
                i = stage_load(wada_d[l], 0, cb * 256)
                ps, pb = nextps()
                for kc in range(KC):
                    A("tensor", lambda e, ps=ps, i=i, kc=kc: e.matmul(ps[0:NSEQ, 0:256], lhsT=cact[:, kc, :], rhs=pf_ap[i][:, kc, :],
                                                                      start=(kc == 0), stop=(kc == KC - 1)),
                      reads=PF[i] + [CONST], writes=[pb])
                r = cb % 2
                A("vector", lambda e, ps=ps, r=r: e.tensor_copy(out=rowt[:, r, :], in_=ps[0:NSEQ, 0:256]), reads=[pb], writes=[ROWB[r]])
                for j in range(2):
                    col = (cb * 2 + j) * NSEQ
                    A("tensor", lambda e, psT=psT, r=r, j=j, col=col: e.transpose(out=psT[:, col:col + NSEQ], in_=rowt[:, r, j * 128:(j + 1) * 128],
                                                                                identity=ident[0:NSEQ, 0:NSEQ]),
                      reads=[ROWB[r], CONST], writes=[pbT])
            A("vector", lambda e, psT=psT, l=l: e.tensor_tensor(out=modT[:, l, :, :], in0=psT[:, 0:24 * NSEQ].rearrange("p (c s) -> p c s", s=NSEQ),
                                                                in1=bT[:, l, :].unsqueeze(2).broadcast_to([128, 24, NSEQ]), op=ALU.add),
              reads=[pbT, VEC], writes=[MODB])
        for l in range(2):
            for s in range(NSEQ):
                A("vector", lambda e, l=l, s=s: e.scalar_tensor_tensor(out=gsT[:, l, :, s], in0=modT[:, l, 8:16, s], scalar=1.0,
                                                                       in1=ngT[:, l, :], op0=ALU.add, op1=ALU.mult),
                  reads=[MODB, VEC], writes=[CONST])

        w_ctr = [0]
        first_pass = [True]
        converted = set()

        def wload(blk):
            i = w_ctr[0] % NW
            w_ctr[0] += 1
            if first_pass[0] and blk not in converted:
                converted.add(blk)
                for hf in range(2):
                    W2d, r0, c0 = blocks[blk][hf]
                    src = W2d[r0:r0 + 1024, c0:c0 + 256].rearrange("(kc p) c -> p kc c", p=128)
                    A("gpsimd", lambda e, src=src, hf=hf: e.dma_start(out=wring[:, i, :, hf * 256:(hf + 1) * 256], in_=src),
                      writes=[WBh[i][hf]], dma_sem="wc%d_%d" % (i, hf))
                A("sync", lambda e: e.dma_start(out=wsc[blk], in_=wring[:, i, :, :]), reads=WBh[i], writes=[WSC[blk]], dma_sem="wst%d" % i)
            else:
                A("sync", lambda e: e.dma_start(out=wring[:, i, :, :], in_=wsc[blk]), reads=[WSC[blk]], writes=WBh[i], dma_sem="w%d" % i)
            return wring[:, i, :, :], WBh[i]

        def hprep(l, s, par):
            for stl in range(NST):
                hprep_pre(l, s, par, stl)
                hprep_rest(l, s, par, stl)

        def hprep_pre(l, s, par, stl, src=None, srcb=None, xn_i=None):
            if True:
                if xn_i is None:
                    xn_i = 12 + 2 * stl
                if src is None:
                    src, srcb = xt[:, stl, :], [XB[stl]]
                xn = T2(xn_i)
                xnb = [TB[xn_i], TB[xn_i + 1]]
                A("scalar", lambda e, stl=stl, xn=xn: e.activation(out=xn, in_=src, func=AF.Square,
                                                                   accum_out=ss[:, par, stl:stl + 1]),
                  reads=srcb, writes=xnb + [SSB[par][stl]])
                A("gpsimd", lambda e, stl=stl: e.tensor_scalar(out=msq[:, par, stl:stl + 1], in0=ss[:, par, stl:stl + 1],
                                                               scalar1=1.0 / D, scalar2=EPS, op0=ALU.mult, op1=ALU.add),
                  reads=[SSB[par][stl]], writes=[SSB[par][stl]])
                A("gpsimd", lambda e, stl=stl: e.tensor_tensor(out=rstd[:, par, stl:stl + 1], in0=msq[:, par, stl:stl + 1],
                                                               in1=mhalf[:, 0:1], op=ALU.pow),
                  reads=[SSB[par][stl], CONST], writes=[SSB[par][stl]])
                A("vector", lambda e, stl=stl, xn=xn: e.tensor_scalar(out=xn, in0=src, scalar1=rstd[:, par, stl:stl + 1],
                                                                      scalar2=None, op0=ALU.mult),
                  reads=srcb + [SSB[par][stl]], writes=xnb)

        def hprep_rest(l, s, par, stl, pool="all", xn_i=None):
            if True:
                if xn_i is None:
                    xn_i = 12 + 2 * stl
                xn = T2(xn_i)
                xnb = [TB[xn_i], TB[xn_i + 1]]
                for half in range(2):
                    ps, pb = nextps(pool)
                    for j in range(4):
                        kc = half * 4 + j
                        A("tensor", lambda e, ps=ps, j=j, kc=kc, xn=xn: e.transpose(out=ps[:, j * 128:(j + 1) * 128],
                                                                                    in_=xn[:, kc * 128:(kc + 1) * 128], identity=ident[:]),
                          reads=xnb + [CONST], writes=[pb])
                    for j in range(4):
                        kc = half * 4 + j
                        if j % 2 == 0:
                            A("vector", lambda e, ps=ps, j=j, kc=kc, stl=stl: e.tensor_scalar(
                                out=hT[:, kc, stl * 128:(stl + 1) * 128], in0=ps[:, j * 128:(j + 1) * 128],
                                scalar1=gsT[:, l, kc, s:s + 1], scalar2=modT[:, l, kc, s:s + 1], op0=ALU.mult, op1=ALU.add),
                              reads=[pb, CONST, MODB], writes=[HB[stl]])
                        else:
                            A("scalar", lambda e, ps=ps, j=j, kc=kc, stl=stl: e.activation(
                                out=hT[:, kc, stl * 128:(stl + 1) * 128], in_=ps[:, j * 128:(j + 1) * 128], func=AF.Identity,
                                scale=gsT[:, l, kc, s:s + 1], bias=modT[:, l, kc, s:s + 1]),
                              reads=[pb, CONST, MODB], writes=[HB[stl]])

        def outproj_evac(l, ps, pb, ch, stl):
            ti = 4 + (stl % 2)
            A("vector", lambda e: e.tensor_tensor(out=T(ti), in0=ps[:], in1=gate_bc[:, l, ch * 512:(ch + 1) * 512], op=ALU.mult),
              reads=[pb, GTB[l]], writes=[TB[ti]])
            A("gpsimd", lambda e: e.tensor_tensor(out=xt[:, stl, ch * 512:(ch + 1) * 512], in0=xt[:, stl, ch * 512:(ch + 1) * 512], in1=T(ti), op=ALU.add),
              reads=[TB[ti], XB[stl]], writes=[XB[stl]])

        def outproj_ch0(l, OB):
            pss = [nextps() for _ in range(NST)]
            for kh in range(2):
                wb, wbuf = wload(OB[(kh, 0)])
                for stl in range(NST):
                    ps, pb = pss[stl]
                    for kc in range(KC):
                        fc = kh * 8 + kc
                        A("tensor", lambda e, ps=ps, wb=wb, kc=kc, fc=fc, stl=stl, kh=kh: e.matmul(
                            ps[:], lhsT=yT[:, fc, stl * 128:(stl + 1) * 128], rhs=wb[:, kc, :],
                            start=(kh == 0 and kc == 0), stop=(kh == 1 and kc == KC - 1)),
                          reads=[YB[fc]] + wbuf, writes=[pb])
            w1 = [wload(OB[(kh, 1)]) for kh in range(2)]
            for stl in range(NST):
                ps, pb = pss[stl]
                outproj_evac(l, ps, pb, 0, stl)
            return w1

        def outproj_ch1_st(l, w1, stl):
            ps, pb = nextps()
            for kh in range(2):
                wb, wbuf = w1[kh]
                for kc in range(KC):
                    fc = kh * 8 + kc
                    A("tensor", lambda e, ps=ps, wb=wb, kc=kc, fc=fc, kh=kh: e.matmul(
                        ps[:], lhsT=yT[:, fc, stl * 128:(stl + 1) * 128], rhs=wb[:, kc, :],
                        start=(kh == 0 and kc == 0), stop=(kh == 1 and kc == KC - 1)),
                      reads=[YB[fc]] + wbuf, writes=[pb])
            outproj_evac(l, ps, pb, 1, stl)

        def boundary(l, OB, fin_st, pre_st, rest_st, defer=False):
            w1 = outproj_ch0(l, OB)
            a_ = [record(outproj_ch1_st, l, w1, stl)[0] for stl in range(NST)]
            if fin_st is None:
                pr = [record(pre_st, stl)[0] for stl in range(NST)]
                rs = [record(rest_st, stl)[0] for stl in range(NST)]
                emit(a_[0])
                emit(merge([a_[1], pr[0]]))
                emit(merge([a_[2], pr[1]]))
                emit(merge([a_[3], pr[2]]))
                emit(pr[3])
                emit(merge(rs))
            else:
                fn = [record(fin_st, stl)[0] for stl in range(NST)]
                ch = [record(pre_st, stl)[0] + record(rest_st, stl)[0] for stl in range(NST)]
                emit(a_[0])
                emit(merge([a_[1], fn[0]]))
                emit(merge([a_[2], fn[1]]))
                emit(merge([a_[3], fn[2]]))
                emit(fn[3])
                if defer:
                    return ch
                emit(merge(ch))
            return []

        def layer0(s, par):
            fence()
            l0_vhalf(0)
            l0_vhalf(1)
            l0_ug()

        def l0_vhalf(half, pool="all"):
            if True:
                for vb in range(4):
                    wb, wbuf = wload(L0_V[vb])
                    for q in range(2):
                        stl = half * 2 + q
                        vbig_i = 6 + 4 * q
                        ps, pb = nextps(pool)
                        for kc in range(KC):
                            A("tensor", lambda e, ps=ps, wb=wb, kc=kc, stl=stl: e.matmul(
                                ps[:], lhsT=hT[:, kc, stl * 128:(stl + 1) * 128], rhs=wb[:, kc, :], start=(kc == 0), stop=(kc == KC - 1)),
                              reads=[HB[stl]] + wbuf, writes=[pb])
                        A("scalar", lambda e, ps=ps, vbig_i=vbig_i, vb=vb: e.activation(out=T(vbig_i + vb), in_=ps[:], func=AF.Gelu_apprx_tanh),
                          reads=[pb], writes=[TB[vbig_i + vb]])
                        A("vector", lambda e, vbig_i=vbig_i, vb=vb, q=q: e.bn_stats(out=bnst[:, q, vb, :], in_=T(vbig_i + vb)),
                          reads=[TB[vbig_i + vb]], writes=[BNB[q]])
                for q in range(2):
                    stl = half * 2 + q
                    vbig_i = 6 + 4 * q
                    A("vector", lambda e, q=q: e.bn_aggr(out=mv[:, q, :], in_=bnst[:, q, :, :].rearrange("p a b -> p (a b)")),
                      reads=[BNB[q]], writes=[BNB[q]])
                    A("gpsimd", lambda e, q=q: e.tensor_scalar(out=vtmp[:, q, :], in0=mv[:, q, 1:2], scalar1=EPS, scalar2=None, op0=ALU.add),
                      reads=[BNB[q]], writes=[BNB[q]])
                    A("gpsimd", lambda e, q=q: e.tensor_tensor(out=vrs[:, q, :], in0=vtmp[:, q, :], in1=mhalf[:, 0:1], op=ALU.pow),
                      reads=[BNB[q], CONST], writes=[BNB[q]])
                    A("vector", lambda e, q=q, stl=stl, vbig_i=vbig_i: e.tensor_scalar(
                        out=vhat[:, stl, :], in0=T4(vbig_i), scalar1=mv[:, q, 0:1], scalar2=vrs[:, q, :], op0=ALU.subtract, op1=ALU.mult),
                      reads=TB[vbig_i:vbig_i + 4] + [BNB[q]], writes=[VHB[stl]])
        def l0_ug():
            for fg in range(4):
                ub, ubuf = wload(L0_U[fg])
                gb, gbuf = wload(L0_G[fg])
                for j in range(4):
                    fc = fg * 4 + j
                    g = fc // 2
                    r = fc % 2
                    iu, itg, isg, is_, iy = 14 + r, 16 + r, 18 + r, 20 + r, 22 + r
                    psu, pbu = nextps()
                    for kc in range(KC):
                        A("tensor", lambda e, psu=psu, ub=ub, kc=kc, j=j: e.matmul(psu[:], lhsT=ub[:, kc, j * 128:(j + 1) * 128], rhs=hT[:, kc, :],
                                                                                 start=(kc == 0), stop=(kc == KC - 1)),
                          reads=HB + ubuf, writes=[pbu])
                    A("scalar", lambda e, psu=psu, iu=iu: e.activation(out=T(iu), in_=psu[:], func=AF.Gelu_apprx_tanh), reads=[pbu], writes=[TB[iu]])
                    psg, pbg = nextps()
                    for kc in range(KC):
                        A("tensor", lambda e, psg=psg, gb=gb, kc=kc, j=j: e.matmul(psg[:], lhsT=gb[:, kc, j * 128:(j + 1) * 128], rhs=hT[:, kc, :],
                                                                                 start=(kc == 0), stop=(kc == KC - 1)),
                          reads=HB + gbuf, writes=[pbg])
                    A("scalar", lambda e, psg=psg, itg=itg: e.activation(out=T(itg), in_=psg[:], func=AF.Tanh, scale=0.5), reads=[pbg], writes=[TB[itg]])
                    A("vector", lambda e, psg=psg, itg=itg, isg=isg: e.scalar_tensor_tensor(out=T(isg), in0=T(itg), scalar=1.0, in1=psg[:],
                                                                                            op0=ALU.add, op1=ALU.mult),
                      reads=[pbg, TB[itg]], writes=[TB[isg]])
                    pss, pbs = nextps()
                    for stl in range(NST):
                        A("tensor", lambda e, pss=pss, stl=stl, fc=fc, g=g: e.matmul(
                            pss[:, stl * 128:(stl + 1) * 128], lhsT=vhat[:, stl, fc * 128:(fc + 1) * 128], rhs=wsT_bf[:, g, :], start=True, stop=True),
                          reads=[VHB[stl], CONST], writes=[pbs])
                    A("vector", lambda e, pss=pss, is_=is_, fc=fc: e.scalar_tensor_tensor(
                        out=T(is_).rearrange("p (a b) -> p a b", b=128), in0=pss[:].rearrange("p (a b) -> p a b", b=128),
                        scalar=lngh[:, fc:fc + 1], in1=Ch[:, fc:fc + 1, :].broadcast_to([128, NST, 128]), op0=ALU.mult, op1=ALU.add),
                      reads=[pbs, CONST], writes=[TB[is_]])
                    A("gpsimd", lambda e, is_=is_, iu=iu, iy=iy: e.tensor_tensor(out=T(iy), in0=T(is_), in1=T(iu), op=ALU.mult),
                      reads=[TB[is_], TB[iu]], writes=[TB[iy]])
                    A("vector", lambda e, iy=iy, isg=isg, fc=fc: e.tensor_tensor(out=yT[:, fc, :], in0=T(iy), in1=T(isg), op=ALU.mult),
                      reads=[TB[iy], TB[isg]], writes=[YB[fc]])
                    if fc == 0:
                        dump("u0", T(iu), [TB[iu]])
                        dump("sg0", T(isg), [TB[isg]])
                        dump("s0", T(is_), [TB[is_]])
            dump("yT", yT[:], YB, BF16)
            dump("gate_bc", gate_bc[:], GTB)
            dump("Ch", Ch[:], [CONST])

        LN_HALF = math.log(0.5)

        def layer1(s, par, nxt=None):
            fence()

            def tq_(i, hh):
                return 6 + 6 * (i % 3) + hh

            def tf_(i, hh):
                return 8 + 6 * (i % 3) + hh

            def tg_(i, hh):
                return 10 + 6 * (i % 3) + hh

            def X(hp):
                p3 = hp % 3
                ab, abuf = wload(L1_A[hp])
                bb, bbuf = wload(L1_B[hp])
                for hh in range(2):
                    for (blkap, blkbuf, c0, slot, stt) in ((ab, abuf, hh * 128, tq_(hp, hh), True),
                                                          (ab, abuf, 256 + hh * 128, tf_(hp, hh), False),
                                                          (bb, bbuf, hh * 128, tg_(hp, hh), True)):
                        ps, pb = nextps("X")
                        for kc in range(KC):
                            A("tensor", lambda e, ps=ps, blkap=blkap, kc=kc, c0=c0: e.matmul(ps[:], lhsT=blkap[:, kc, c0:c0 + 128], rhs=hT[:, kc, :],
                                                                                         start=(kc == 0), stop=(kc == KC - 1)),
                              reads=HB + blkbuf, writes=[pb])
                        A("scalar", lambda e, ps=ps, slot=slot: e.activation(out=T(slot), in_=ps[:], func=AF.Tanh, scale=0.5), reads=[pb], writes=[TB[slot]])
                        if stt:
                            A("vector", lambda e, ps=ps, slot=slot: e.scalar_tensor_tensor(out=T(slot), in0=T(slot), scalar=1.0, in1=ps[:],
                                                                                           op0=ALU.add, op1=ALU.mult),
                              reads=[pb, TB[slot]], writes=[TB[slot]])
                for stl in range(NST):
                    ps, pb = nextps("X")
                    for kc in range(KC):
                        A("tensor", lambda e, ps=ps, bb=bb, kc=kc, stl=stl: e.matmul(
                            ps[:, 0:256], lhsT=hT[:, kc, stl * 128:(stl + 1) * 128], rhs=bb[:, kc, 256:512], start=(kc == 0), stop=(kc == KC - 1)),
                          reads=[HB[stl]] + bbuf, writes=[pb])
                    A("scalar", lambda e, ps=ps, stl=stl, p3=p3: e.activation(out=v_tok[:, p3, stl, :], in_=ps[:, 0:256], func=AF.Copy),
                      reads=[pb], writes=[VTB[p3][stl]])

            def YL(hp):
                for hh in range(2):
                    h = 2 * hp + hh
                    itf = tf_(hp, hh)
                    A("scalar", lambda e, hh=hh, h=h, itf=itf: e.activation(out=T(24 + hh), in_=T(itf), func=AF.Ln, scale=c1[:, h:h + 1], bias=c2[:, h:h + 1]),
                      reads=[TB[itf], CONST], writes=[TB[24 + hh]])
                    A("gpsimd", lambda e, hh=hh, h=h, itf=itf: e.tensor_scalar(out=T(itf), in0=T(itf), scalar1=nc1[:, h:h + 1], scalar2=c1[:, h:h + 1],
                                                                               op0=ALU.mult, op1=ALU.add),
                      reads=[TB[itf], CONST], writes=[TB[itf]])

            def Pfx(hp, hh):
                p2 = hp % 2
                iL, iA = 24 + hh, 26 + hh
                iq, ik = tq_(hp, hh), tf_(hp, hh)
                cl = CLB[p2][hh]
                A("vector", lambda e: e.tensor_tensor_scan(out=T(iA), data0=mask01[:], data1=T(iL), initial=0.0, op0=ALU.mult, op1=ALU.add),
                  reads=[TB[iL], CONST], writes=[TB[iA]])
                a3 = T(iA).rearrange("p (c t) -> p c t", t=128)
                A("gpsimd", lambda e: e.tensor_tensor(out=T(iL).rearrange("p (c t) -> p c t", t=128), in0=a3,
                                                      in1=a3[:, :, 63:64].broadcast_to([128, NST, 128]), op=ALU.subtract),
                  reads=[TB[iA]], writes=[TB[iL]])
                A("vector", lambda e: e.tensor_tensor(out=cols[:, p2, hh, 0, :], in0=a3[:, :, 127], in1=a3[:, :, 63], op=ALU.subtract),
                  reads=[TB[iA]], writes=[cl])
                A("scalar", lambda e: e.activation(out=cols[:, p2, hh, 2, :], in_=a3[:, :, 127], func=AF.Exp), reads=[TB[iA], cl], writes=[cl])
                A("scalar", lambda e: e.activation(out=cols[:, p2, hh, 3, :], in_=a3[:, :, 63], func=AF.Exp), reads=[TB[iA], cl], writes=[cl])
                A("scalar", lambda e: e.activation(out=cols[:, p2, hh, 4, :], in_=cols[:, p2, hh, 0, :], func=AF.Exp), reads=[cl], writes=[cl])
                A("scalar", lambda e: e.activation(out=T(iA), in_=T(iL), func=AF.Exp, bias=LN_HALF), reads=[TB[iL], TB[iA]], writes=[TB[iA]])
                A("scalar", lambda e: e.activation(out=T(iL), in_=T(iL), func=AF.Exp, scale=-1.0), reads=[TB[iL]], writes=[TB[iL]])
                A("gpsimd", lambda e: e.tensor_tensor(out=q_inT[:, p2, hh, :], in0=T(iq), in1=T(iA), op=ALU.mult),
                  reads=[TB[iq], TB[iA]], writes=[QIB[p2][hh]])
                A("vector", lambda e: e.tensor_tensor(out=k_inT[:, p2, hh, :], in0=T(ik), in1=T(iL), op=ALU.mult),
                  reads=[TB[ik], TB[iL]], writes=[KIB[p2][hh]])

            def Mid(hp, hh):
                p2 = hp % 2
                p3 = hp % 3
                h = 2 * hp + hh
                cl = CLB[p2][hh]
                qib, kib = QIB[p2][hh], KIB[p2][hh]
                bankA = (psum[4 + 2 * hh], PB[4 + 2 * hh])
                bankB = (psum[5 + 2 * hh], PB[5 + 2 * hh])
                ps, pb = bankA
                psb = ps[:].bitcast(BF16)
                pssc, pbsc = bankB
                psU, pbU = bankA
                pso, pbo = bankB
                psm, pbm = bankA
                for stl in range(NST):
                    A("tensor", lambda e, psb=psb, stl=stl: e.transpose(out=psb[:, stl * 128:(stl + 1) * 128], in_=k_inT[:, p2, hh, stl * 128:(stl + 1) * 128],
                                                                     identity=ident_bf[:]), reads=[kib, CONST], writes=[pb])
                for stl in range(NST):
                    A("tensor", lambda e, pssc=pssc, stl=stl: e.matmul(pssc[:, stl * 128:(stl + 1) * 128], lhsT=k_inT[:, p2, hh, stl * 128:(stl + 1) * 128],
                                                                     rhs=q_inT[:, p2, hh, stl * 128:(stl + 1) * 128], start=True, stop=True),
                      reads=[kib, qib], writes=[pbsc])
                A("scalar", lambda e, psb=psb: e.activation(out=k_tok[:, hh, :, :].rearrange("p b c -> p (b c)"), in_=psb[:, 0:NST * 128], func=AF.Copy),
                  reads=[pb], writes=[KTB[hh]])
                A("vector", lambda e, pssc=pssc: e.tensor_tensor(out=scm[:, hh, :, :], in0=pssc[:].rearrange("p (a b) -> p a b", b=128),
                                                                 in1=cmask[:].unsqueeze(1).broadcast_to([128, NST, 128]), op=ALU.mult),
                  reads=[pbsc, CONST], writes=[SCB[hh]])
                A("gpsimd", lambda e: e.tensor_scalar(out=stq[:, hh, 0, :], in0=state[:, h, :], scalar1=cols[:, p2, hh, 3, 0:1], scalar2=1.0, op0=ALU.mult, op1=ALU.mult),
                  reads=[STB[h], cl], writes=[SQTB[hh]])
                for stl in range(NST):
                    A("tensor", lambda e, psU=psU, stl=stl: e.matmul(psU[:, stl * 128:(stl + 1) * 128], lhsT=k_tok[:, hh, stl, :],
                                                                   rhs=v_tok[:, p3, stl, hh * 128:(hh + 1) * 128], start=True, stop=True),
                      reads=[KTB[hh], VTB[p3][stl]], writes=[pbU])
                A("vector", lambda e, psU=psU: e.tensor_tensor(out=utmp[:, hh, :, :], in0=psU[:].rearrange("p (a b) -> p a b", b=128),
                                                               in1=cols[:, p2, hh, 4, :].unsqueeze(2).broadcast_to([128, NST, 128]), op=ALU.mult),
                  reads=[pbU, cl], writes=[UTB[hh]])
                for stl in range(NST):
                    src = state[:, h, :] if stl == 0 else Sh[:, hh, stl - 1, :]
                    dst = state[:, h, :] if stl == NST - 1 else Sh[:, hh, stl, :]
                    rd = [UTB[hh], cl] + ([STB[h]] if stl == 0 else [SHB[hh]])
                    wr = [STB[h]] if stl == NST - 1 else [SHB[hh]]
                    A("vector", lambda e, src=src, dst=dst, stl=stl: e.scalar_tensor_tensor(
                        out=dst, in0=src, scalar=cols[:, p2, hh, 2, stl:stl + 1], in1=utmp[:, hh, stl, :], op0=ALU.mult, op1=ALU.add),
                      reads=rd, writes=wr)
                A("gpsimd", lambda e: e.tensor_tensor(out=stq[:, hh, 1:NST, :], in0=Sh[:, hh, :, :],
                                                      in1=cols[:, p2, hh, 3, 1:NST].unsqueeze(2).broadcast_to([128, NST - 1, 128]), op=ALU.mult),
                  reads=[SHB[hh], cl, SQTB[hh]], writes=[SQTB[hh]])
                for stl in range(NST):
                    A("tensor", lambda e, pso=pso, stl=stl: e.matmul(pso[:, stl * 128:(stl + 1) * 128], lhsT=v_tok[:, p3, stl, hh * 128:(hh + 1) * 128],
                                                                   rhs=scm[:, hh, stl, :], start=True, stop=False),
                      reads=[VTB[p3][stl], SCB[hh]], writes=[pbo])
                    A("tensor", lambda e, pso=pso, stl=stl: e.matmul(pso[:, stl * 128:(stl + 1) * 128], lhsT=stq[:, hh, stl, :],
                                                                   rhs=q_inT[:, p2, hh, stl * 128:(stl + 1) * 128], start=False, stop=True),
                      reads=[SQTB[hh], qib], writes=[pbo])
                A("scalar", lambda e, pso=pso: e.activation(out=sqb[:, hh, :], in_=pso[:], func=AF.Square), reads=[pbo], writes=[SQB[hh]])
                A("tensor", lambda e, psm=psm: e.matmul(psm[:], lhsT=onesm_bf[:], rhs=sqb[:, hh, :], start=True, stop=True),
                  reads=[SQB[hh], CONST], writes=[pbm])

            def TailLn(hp, hh):
                psm, pbm = psum[4 + 2 * hh], PB[4 + 2 * hh]
                A("scalar", lambda e: e.activation(out=T(28 + hh), in_=psm[:], func=AF.Ln, bias=EPS), reads=[pbm], writes=[TB[28 + hh]])

            def TailRest(hp, hh):
                h = 2 * hp + hh
                pso, pbo = psum[5 + 2 * hh], PB[5 + 2 * hh]
                irs, ion, ig = 28 + hh, 30 + hh, tg_(hp, hh)
                A("scalar", lambda e: e.activation(out=T(irs), in_=T(irs), func=AF.Exp, scale=-0.5), reads=[TB[irs]], writes=[TB[irs]])
                A("vector", lambda e: e.tensor_tensor(out=T(ion), in0=pso[:], in1=T(irs), op=ALU.mult), reads=[pbo, TB[irs]], writes=[TB[ion]])
                A("vector", lambda e: e.scalar_tensor_tensor(out=yT[:, h, :], in0=T(ion), scalar=gngh[:, 0:1], in1=T(ig), op0=ALU.mult, op1=ALU.mult),
                  reads=[TB[ion], TB[ig], CONST], writes=[YB[h]])

            def both(f, hp):
                return merge([record(f, hp, 0)[0], record(f, hp, 1)[0]])

            for i in range(-2, 8):
                lists = []
                if i + 2 < 8:
                    lists.append(record(X, i + 2)[0])
                if 0 <= i + 1 < 8:
                    lists.append(both(Pfx, i + 1))
                if i >= 0:
                    lists.append(both(Mid, i))
                if nxt is not None and i == 6:
                    row1, s2 = nxt
                    for st_ in range(2):
                        def pf(st_=st_):
                            hprep_pre(0, s2, 0, st_, src=T2(2 * st_), srcb=[TB[2 * st_], TB[2 * st_ + 1]], xn_i=18 + 2 * st_)
                            hprep_rest(0, s2, 0, st_, pool="X", xn_i=18 + 2 * st_)
                        lists.append(record(pf)[0])
                emit(merge(lists))
                if i >= 0:
                    emit(both(TailLn, i))
                if i + 2 < 8:
                    YL(i + 2)
                if i >= 0:
                    emit(both(TailRest, i))
                if nxt is not None and i == 0:
                    row1, s2 = nxt
                    for st_ in range(2):
                        A("sync", lambda e, st_=st_, row1=row1: e.dma_start(out=T2(2 * st_), in_=x_d[row1 + st_ * 128: row1 + (st_ + 1) * 128, :]),
                          writes=[TB[2 * st_], TB[2 * st_ + 1]], dma_sem="xp%d" % st_)
            dump("yT1", yT[:], YB, BF16)

        def gate_setup(s):
            for l in range(2):
                for ch in range(2):
                    ps, pb = nextps()
                    for j in range(4):
                        kc = ch * 4 + j
                        A("vector", lambda e, l=l, kc=kc, j=j: e.tensor_scalar(out=T(4)[:, j * 128:(j + 1) * 128], in0=ident[:],
                                                                              scalar1=modT[:, l, 16 + kc, s:s + 1], scalar2=None, op0=ALU.mult),
                          reads=[MODB, CONST], writes=[TB[4]])
                    A("tensor", lambda e, ps=ps: e.matmul(ps[:], lhsT=ones_f[:], rhs=T(4), start=True, stop=True),
                      reads=[TB[4], CONST], writes=[pb])
                    A("vector", lambda e, ps=ps, l=l, ch=ch: e.tensor_copy(out=gate_bc[:, l, ch * 512:(ch + 1) * 512], in_=ps[:]), reads=[pb], writes=[GTB[l]])

        def xload_st(row0, stl):
            A("sync", lambda e: e.dma_start(out=xt[:, stl, :], in_=x_d[row0 + stl * 128: row0 + (stl + 1) * 128, :]),
              writes=[XB[stl]], dma_sem="xload%d" % stl)

        def final_st(row0, stl):
            oi = stl % 2
            ot = T2(24 + 2 * oi)
            otb = [TB[24 + 2 * oi], TB[25 + 2 * oi]]
            A("scalar", lambda e: e.activation(out=ot, in_=xt[:, stl, :], func=AF.Square, accum_out=ss[:, 2, stl:stl + 1]),
              reads=[XB[stl]], writes=otb + [SSB[2][stl]])
            A("gpsimd", lambda e: e.tensor_scalar(out=msq[:, 2, stl:stl + 1], in0=ss[:, 2, stl:stl + 1], scalar1=1.0 / D, scalar2=EPS,
                                                  op0=ALU.mult, op1=ALU.add), reads=[SSB[2][stl]], writes=[SSB[2][stl]])
            A("gpsimd", lambda e: e.tensor_tensor(out=rstd[:, 2, stl:stl + 1], in0=msq[:, 2, stl:stl + 1], in1=mhalf[:, 0:1], op=ALU.pow),
              reads=[SSB[2][stl], CONST], writes=[SSB[2][stl]])
            A("vector", lambda e: e.scalar_tensor_tensor(out=ot, in0=xt[:, stl, :], scalar=rstd[:, 2, stl:stl + 1], in1=fgain_bc[:],
                                                         op0=ALU.mult, op1=ALU.mult),
              reads=[XB[stl], SSB[2][stl], FGB], writes=otb)
            A("sync", lambda e: e.dma_start(out=out_d[row0 + stl * 128: row0 + (stl + 1) * 128, :], in_=ot),
              reads=otb, writes=[OUTB[oi]], dma_sem="ost%d" % oi)

        tiles = [(s, j) for s in range(NSEQ) for j in range(NSUP)]
        deferred = []
        for ti_, (s, j) in enumerate(tiles):
            row0 = s * SEQ + j * NT
            if ti_ == 0:
                for stl in range(NST):
                    xload_st(row0, stl)
                hprep(0, s, 0)
            if j == 0:
                gate_setup(s)
                if s > 0:
                    A("gpsimd", lambda e: e.memset(state[:], 0.0), reads=STB, writes=STB)
            fence()
            if ti_ == 1:
                dump("hT_t1", hT[:], HB, BF16)
                dump("xpre", tpool[:, 0:4, :], TB[0:4])
                dump("rstd_t1", rstd[:], [b for r in SSB for b in r])
            if deferred:
                emit(merge([record(l0_vhalf, 0, "Y")[0]] + deferred))
            else:
                l0_vhalf(0)
            l0_vhalf(1)
            l0_ug()
            boundary(0, L0_O, None, lambda stl, s=s: hprep_pre(1, s, 1, stl), lambda stl, s=s: hprep_rest(1, s, 1, stl))
            if DEBUG:
                A("sync", lambda e, row0=row0: e.dma_start(out=dbg_d[row0:row0 + NT, :].rearrange("(a p) d -> p a d", p=128), in_=xt[:]),
                  reads=XB, writes=[DBGB], dma_sem="dbg")
            if ti_ + 1 < len(tiles):
                s2, j2 = tiles[ti_ + 1]
                row1 = s2 * SEQ + j2 * NT
                layer1(s, 1, (row1, s2))

                def fin(stl, row0=row0, row1=row1):
                    final_st(row0, stl)
                    if stl < 2:
                        A("gpsimd", lambda e: e.tensor_copy(out=xt[:, stl, :], in_=T2(2 * stl)),
                          reads=[TB[2 * stl], TB[2 * stl + 1], XB[stl]], writes=[XB[stl]])
                    else:
                        xload_st(row1, stl)

                def pre(stl, s2=s2):
                    if stl >= 2:
                        hprep_pre(0, s2, 0, stl)

                def rest(stl, s2=s2):
                    if stl >= 2:
                        hprep_rest(0, s2, 0, stl, pool="X")
                chains = boundary(1, L1_O, fin, pre, rest, defer=True)
                deferred = [c for c in chains if c]
            else:
                layer1(s, 1, None)

                def fin(stl, row0=row0):
                    final_st(row0, stl)
                boundary(1, L1_O, fin, lambda stl: None, lambda stl: None)
                deferred = []
            first_pass[0] = False
        A("sync", None, reads=OUTB + ([DBGB] if DEBUG else []), writes=OUTB)
        P.finalize(st)
    return nc


_CACHE = {}


def _get_nc(NSEQ, SEQ):
    key = (NSEQ, SEQ)
    if key not in _CACHE:
        _CACHE[key] = build(NSEQ, SEQ)
    return _CACHE[key]


def run_cores(inputs, n_cores, NSEQ, SEQ):
    f32 = lambda a: np.ascontiguousarray(np.asarray(a, dtype=np.float32))
    x = f32(inputs["x"])
    c = f32(inputs["c"])
    shared = {k: f32(inputs[k]) for k in ("norm_gain", "w_ada", "b_ada", "a_w_in", "a_ln_gain", "a_ln_bias", "a_w_s", "a_b_s",
                                          "a_w_out", "b_w_in", "b_lower_bounds", "b_gn_gain", "b_w_out", "final_gain")}
    in_maps = []
    for i in range(n_cores):
        m = dict(shared)
        m["x"] = np.ascontiguousarray(x[i * NSEQ:(i + 1) * NSEQ].reshape(NSEQ * SEQ, D))
        m["c"] = np.ascontiguousarray(c[i * NSEQ:(i + 1) * NSEQ])
        in_maps.append(m)
    nc = _get_nc(NSEQ, SEQ)
    res = run_bass_kernel_spmd(nc, in_maps, core_ids=list(range(n_cores)))
    outs = [np.asarray(r["out"]).reshape(NSEQ, SEQ, D) for r in res.results]
    if DEBUG:
        LAST.clear()
        LAST.append(res.results)
    return np.concatenate(outs, axis=0).astype(np.float32)


def kernel(**inputs):
    B, S, _ = inputs["x"].shape
    nseq = B // N_CORES
    return run_cores(inputs, N_CORES, nseq, S)
```

```python
import math
from contextlib import ExitStack

import numpy as np
import concourse.bass as bass
import concourse.mybir as mybir
from concourse.bass_utils import run_bass_kernel_spmd

F32 = mybir.dt.float32
BF16 = mybir.dt.bfloat16
AF = mybir.ActivationFunctionType
ALU = mybir.AluOpType

ENGS = ("sync", "scalar", "vector", "gpsimd", "tensor")

D = 1024
DI = 2048
KC = 8
NT = 512
NST = 4
EPS = 1e-6
N_CORES = 8
DEBUG = False
DUMPNAMES = []
LAST = []


class Buf:
    __slots__ = ("name", "last_w", "readers")

    def __init__(self, name):
        self.name = name
        self.last_w = None
        self.readers = []


class Op:
    __slots__ = ("eng", "fn", "idx", "deps", "dma_sem", "signal", "sigcount", "waits", "known", "gidx")

    def __init__(self, eng, fn, dma_sem):
        self.eng = eng
        self.fn = fn
        self.dma_sem = dma_sem
        self.deps = []
        self.signal = False
        self.sigcount = 0
        self.waits = []
        self.known = None


class Prog:
    def __init__(self, nc):
        self.nc = nc
        self.ops = []
        self.per_eng = {e: [] for e in ENGS}

    def add(self, eng, fn, reads=(), writes=(), dma_sem=None):
        op = Op(eng, fn, dma_sem)
        op.gidx = len(self.ops)
        op.idx = len(self.per_eng[eng])
        deps = {}
        for b in reads:
            if b.last_w is not None:
                deps[id(b.last_w)] = b.last_w
        for b in writes:
            if b.last_w is not None:
                deps[id(b.last_w)] = b.last_w
            for r in b.readers:
                deps[id(r)] = r
        op.deps = list(deps.values())
        for b in reads:
            b.readers.append(op)
        for b in writes:
            b.last_w = op
            b.readers = []
        self.ops.append(op)
        self.per_eng[eng].append(op)
        return op

    def finalize(self, stack):
        nc = self.nc
        dma_cum = {}
        for op in self.ops:
            if op.dma_sem is not None:
                dma_cum[op.dma_sem] = dma_cum.get(op.dma_sem, 0) + 16
                op.sigcount = dma_cum[op.dma_sem]
        eng_known = {e: {} for e in ENGS}
        for op in self.ops:
            kn = eng_known[op.eng]
            waits = []
            for p in sorted(op.deps, key=lambda q: -q.gidx):
                if p.dma_sem is not None:
                    key = ("d", p.dma_sem)
                    val = p.sigcount
                else:
                    key = p.eng
                    val = p.idx + 1
                    if p.eng == op.eng:
                        if op.eng == "tensor" or op.eng == "sync":
                            continue
                if kn.get(key, 0) >= val:
                    continue
                waits.append(p)
                kn[key] = val
                if p.known is not None:
                    for k2, v2 in p.known.items():
                        if kn.get(k2, 0) < v2:
                            kn[k2] = v2
            op.waits = waits
            for p in waits:
                p.signal = True
            kn2 = dict(kn)
            if op.dma_sem is None:
                kn2[op.eng] = op.idx + 1
            op.known = kn2
        for e in ENGS:
            c = 0
            for op in self.per_eng[e]:
                if op.dma_sem is None and op.signal:
                    c += 1
                    op.sigcount = c
        sems = {}
        for e in ENGS:
            sems[e] = stack.enter_context(nc.semaphore("sem_" + e))
        for k in dma_cum:
            sems[("d", k)] = stack.enter_context(nc.semaphore("semd_" + str(k)))
        block = stack.enter_context(nc.Block())

        def emit_engine(eng_name):
            def body(eng):
                for op in self.per_eng[eng_name]:
                    for p in op.waits:
                        if p.dma_sem is not None:
                            eng.wait_ge(sems[("d", p.dma_sem)], p.sigcount)
                        else:
                            eng.wait_ge(sems[p.eng], p.sigcount)
                    if op.fn is None:
                        continue
                    inst = op.fn(eng)
                    if op.dma_sem is not None:
                        inst.then_inc(sems[("d", op.dma_sem)], 16)
                    elif op.signal:
                        inst.then_inc(sems[eng_name], 1)
            return body

        for e in ENGS:
            if self.per_eng[e]:
                getattr(block, e)(emit_engine(e))


def build(NSEQ, SEQ):
    NSUP = SEQ // NT
    NTOK = NSEQ * SEQ
    nc = bass.Bass("TRN2", target_bir_lowering=False)

    def din(name, shape):
        return nc.dram_tensor(name, list(shape), F32, kind="ExternalInput").ap()

    x_d = din("x", [NTOK, D])
    c_d = din("c", [NSEQ, D])
    ng_d = din("norm_gain", [2, D])
    wada_d = din("w_ada", [2, D, 3 * D])
    bada_d = din("b_ada", [2, 3 * D])
    awin_d = din("a_w_in", [1, D, 3 * DI])
    alng_d = din("a_ln_gain", [1, DI])
    alnb_d = din("a_ln_bias", [1, DI])
    aws_d = din("a_w_s", [1, 8, 128, 128])
    abs_d = din("a_b_s", [1, 8, 128])
    awout_d = din("a_w_out", [1, DI, D])
    bwin_d = din("b_w_in", [1, D, 4 * DI])
    blb_d = din("b_lower_bounds", [2, DI])
    bgn_d = din("b_gn_gain", [1, 128])
    bwout_d = din("b_w_out", [1, DI, D])
    fg_d = din("final_gain", [D])
    out_d = nc.dram_tensor("out", [NTOK, D], F32, kind="ExternalOutput").ap()
    dbg_d = nc.dram_tensor("dbg", [NTOK, D], F32, kind="ExternalOutput").ap() if DEBUG else None

    blocks = []

    def defblk(halves):
        blocks.append(halves)
        return len(blocks) - 1

    AW = awin_d[0]
    BW = bwin_d[0]
    L0_V = [defblk([(AW, 0, DI + vb * 512), (AW, 0, DI + vb * 512 + 256)]) for vb in range(4)]
    L0_U = []
    L0_G = []
    for fg in range(4):
        L0_U.append(defblk([(AW, 0, fg * 512), (AW, 0, fg * 512 + 256)]))
        L0_G.append(defblk([(AW, 0, 2 * DI + fg * 512), (AW, 0, 2 * DI + fg * 512 + 256)]))
    L0_O = {}
    for ch in range(2):
        for kh in range(2):
            L0_O[(kh, ch)] = defblk([(awout_d[0], kh * 1024, ch * 512), (awout_d[0], kh * 1024, ch * 512 + 256)])
    L1_A = []
    L1_B = []
    for hp in range(8):
        L1_A.append(defblk([(BW, 0, hp * 256), (BW, 0, DI + hp * 256)]))
        L1_B.append(defblk([(BW, 0, 3 * DI + hp * 256), (BW, 0, 2 * DI + hp * 256)]))
    L1_O = {}
    for ch in range(2):
        for kh in range(2):
            L1_O[(kh, ch)] = defblk([(bwout_d[0], kh * 1024, ch * 512), (bwout_d[0], kh * 1024, ch * 512 + 256)])
    NBLK = len(blocks)
    wsc = nc.dram_tensor("wsc", [NBLK, 128, KC, 512], BF16, kind="Internal").ap()
    WSC = [Buf("wsc%d" % i) for i in range(NBLK)]

    with ExitStack() as st:
        st.enter_context(nc.allow_non_contiguous_dma(reason="small one-time parameter vectors"))

        def sb(name, shape, dt=F32):
            return st.enter_context(nc.sbuf_tensor(name, list(shape), dt))

        P = Prog(nc)
        rec = [None]

        def A(eng, fn, reads=(), writes=(), dma_sem=None):
            if rec[0] is not None:
                rec[0].append((eng, fn, list(reads), list(writes), dma_sem))
            else:
                P.add(eng, fn, reads, writes, dma_sem)

        def record(f, *args):
            old = rec[0]
            rec[0] = []
            ret = f(*args)
            lst = rec[0]
            rec[0] = old
            return lst, ret

        def emit(lst):
            for it in lst:
                A(*it)

        def merge(lists):
            lists = [l for l in lists if l]
            idx = [0] * len(lists)
            out = []
            while True:
                best = None
                for k, l in enumerate(lists):
                    if idx[k] < len(l):
                        frac = (idx[k] + 0.5) / len(l)
                        if best is None or frac < best[0]:
                            best = (frac, k)
                if best is None:
                    break
                k = best[1]
                out.append(lists[k][idx[k]])
                idx[k] += 1
            return out
        OUTB = [Buf("out0"), Buf("out1")]
        DBGB = Buf("dbg")

        NSLOT = 32
        tpool = sb("tpool", [128, NSLOT, 512])
        TB = [Buf("T%d" % i) for i in range(NSLOT)]
        xt = sb("xt", [128, NST, D])
        XB = [Buf("x%d" % i) for i in range(NST)]
        hT = sb("hT", [128, KC, NT], BF16)
        HB = [Buf("h%d" % i) for i in range(NST)]
        NW = 4
        wring = sb("wring", [128, NW, KC, 512], BF16)
        WBh = [[Buf("w%d_%d" % (i, h)) for h in range(2)] for i in range(NW)]
        shared16 = sb("shared16", [128, NST * DI], BF16)
        vhat = shared16[:, :].rearrange("p (a d) -> p a d", d=DI)
        VHB = [Buf("vh%d" % i) for i in range(NST)]
        yT = sb("yT", [128, 16, NT], BF16)
        YB = [Buf("y%d" % i) for i in range(16)]
        gate_bc = sb("gate_bc", [128, 2, D])
        GTB = [Buf("gt%d" % i) for i in range(2)]
        fgain_bc = sb("fgain_bc", [128, D])
        FGB = Buf("fgain")
        state = sb("state", [128, 16, 128])
        STB = [Buf("st%d" % h) for h in range(16)]
        ident = sb("ident", [128, 128])
        ident_bf = sb("ident_bf", [128, 128], BF16)
        ones_f = sb("ones_f", [128, 128])
        onesm_bf = sb("onesm_bf", [128, 128], BF16)
        mask01 = sb("mask01", [128, NT], BF16)
        cmask = sb("cmask", [128, 128])
        wsT_bf = sb("wsT_bf", [128, 8, 128], BF16)
        Ch = sb("Ch", [128, 16, 128])
        CONST = Buf("const")
        ngT = sb("ngT", [128, 2, KC])
        lng = sb("lng", [128, 16])
        lnb = sb("lnb", [128, 16])
        lngh = sb("lngh", [128, 16])
        lnbh = sb("lnbh", [128, 16])
        blb = sb("blb", [128, 2, 16])
        lbt = sb("lbt", [128, 4, 16])
        c1 = sb("c1", [128, 16])
        c2 = sb("c2", [128, 16])
        nc1 = sb("nc1", [128, 16])
        gng = sb("gng", [128, 1])
        gngh = sb("gngh", [128, 1])
        bs_bc = tpool[:, 4:6, :].rearrange("p a (g s) -> p (a g) s", s=128)
        bsh = bs_bc
        cT = sb("cT", [128, KC, NSEQ])
        cact = sb("cact", [128, KC, NSEQ])
        csig = sb("csig", [128, KC, NSEQ])
        bT = sb("bT", [128, 2, 24])
        rowt = sb("rowt", [NSEQ, 2, 256])
        ROWB = [Buf("rowt0"), Buf("rowt1")]
        modT = sb("modT", [128, 2, 24, NSEQ])
        gsT = sb("gsT", [128, 2, KC, NSEQ])
        mhalf = sb("mhalf", [128, 4])
        ss = sb("ss", [128, 3, NST])
        msq = sb("msq", [128, 3, NST])
        rstd = sb("rstd", [128, 3, NST])
        SSB = [[Buf("ss%d_%d" % (p_, i)) for i in range(NST)] for p_ in range(3)]
        bnst = sb("bnst", [128, 2, 4, 6])
        mv = sb("mv", [128, 2, 2])
        vtmp = sb("vtmp", [128, 2, 1])
        vrs = sb("vrs", [128, 2, 1])
        BNB = [Buf("bn%d" % i) for i in range(2)]
        q_inT = shared16[:, 0:2048].rearrange("p (s h t) -> p s h t", s=2, h=2)
        k_inT = shared16[:, 2048:4096].rearrange("p (s h t) -> p s h t", s=2, h=2)
        v_tok = shared16[:, 4096:7168].rearrange("p (s a c) -> p s a c", s=3, a=NST)
        QIB = [[Buf("qi%d_%d" % (p_, i)) for i in range(2)] for p_ in range(2)]
        KIB = [[Buf("ki%d_%d" % (p_, i)) for i in range(2)] for p_ in range(2)]
        VTB = [[Buf("vt%d_%d" % (p_, i)) for i in range(NST)] for p_ in range(3)]
        ALIASED = [b for r in QIB for b in r] + [b for r in KIB for b in r] + [b for r in VTB for b in r]
        fence_t = sb("fence_t", [128, 2])

        def fence():
            A("gpsimd", lambda e: e.memset(fence_t[:], 0.0), writes=VHB + ALIASED)
        k_tok = sb("k_tok", [128, 2, NST, 128], BF16)
        KTB = [Buf("ktok0"), Buf("ktok1")]
        scm = sb("scm", [128, 2, NST, 128], BF16)
        SCB = [Buf("scm%d" % i) for i in range(2)]
        sqb = sb("sqb", [128, 2, NT], BF16)
        SQB = [Buf("sq%d" % i) for i in range(2)]
        Sh = sb("Sh", [128, 2, 3, 128])
        SHB = [Buf("Sh%d" % i) for i in range(2)]
        stq = sb("stq", [128, 2, NST, 128], BF16)
        SQTB = [Buf("stq%d" % i) for i in range(2)]
        utmp = sb("utmp", [128, 2, NST, 128])
        UTB = [Buf("ut%d" % i) for i in range(2)]
        cols = sb("cols", [128, 2, 2, 5, NST])
        CLB = [[Buf("cl%d_%d" % (p_, i)) for i in range(2)] for p_ in range(2)]

        psum = [st.enter_context(nc.psum_tensor("ps%d" % i, [128, 512], F32)) for i in range(8)]
        PB = [Buf("ps%d" % i) for i in range(8)]
        ps_ctr = {"all": 0, "X": 0, "Y": 0}

        def nextps(pool="all"):
            if pool == "all":
                i = ps_ctr[pool] % 7
            elif pool == "X":
                i = ps_ctr[pool] % 4
            else:
                i = 4 + ps_ctr[pool] % 4
            ps_ctr[pool] += 1
            return psum[i], PB[i]

        def T(i):
            return tpool[:, i, :]

        def T2(i):
            return tpool[:, i:i + 2, :].rearrange("p a b -> p (a b)")

        def T4(i):
            return tpool[:, i:i + 4, :].rearrange("p a b -> p (a b)")

        dma_ctr = [0]

        def dkey(prefix="k"):
            dma_ctr[0] += 1
            return "%s%d" % (prefix, dma_ctr[0])

        dumps = {}

        def dump(name, ap, bufs, dt=F32):
            if not DEBUG or name in dumps:
                return
            shp = list(ap.shape)
            d = nc.dram_tensor("dbg_" + name, shp, dt, kind="ExternalOutput").ap()
            dumps[name] = d
            DUMPNAMES.append("dbg_" + name)
            A("sync", lambda e: e.dma_start(out=d, in_=ap), reads=bufs, writes=[DBGB], dma_sem="dbg_" + name)

        A("gpsimd", lambda e: e.memset(ident[:], 1.0), writes=[CONST])
        A("gpsimd", lambda e: e.affine_select(out=ident[:], in_=ident[:], pattern=[[-1, 128]],
                                              compare_op=ALU.is_equal, fill=0.0, base=0, channel_multiplier=1),
          reads=[CONST], writes=[CONST])
        A("gpsimd", lambda e: e.tensor_copy(out=ident_bf[:], in_=ident[:]), reads=[CONST], writes=[CONST])
        A("gpsimd", lambda e: e.memset(ones_f[:], 1.0), writes=[CONST])
        A("gpsimd", lambda e: e.memset(onesm_bf[:], 1.0 / 128.0), writes=[CONST])
        A("gpsimd", lambda e: e.memset(mask01[:], 1.0), writes=[CONST])
        A("gpsimd", lambda e: e.memset(mask01[:].rearrange("p (c t) -> p c t", t=128)[:, :, 0:1], 0.0),
          reads=[CONST], writes=[CONST])
        A("gpsimd", lambda e: e.memset(cmask[:], 1.0), writes=[CONST])
        A("gpsimd", lambda e: e.affine_select(out=cmask[:], in_=cmask[:], pattern=[[1, 128]],
                                              compare_op=ALU.is_ge, fill=0.0, base=0, channel_multiplier=-1),
          reads=[CONST], writes=[CONST])
        A("gpsimd", lambda e: e.memset(mhalf[:], -0.5), writes=[CONST])
        A("gpsimd", lambda e: e.memset(state[:], 0.0), writes=STB)

        def small_load(dst_ap, src_ap, bufs):
            k = dkey("c")
            A("sync", lambda e: e.dma_start(out=dst_ap, in_=src_ap), writes=bufs, dma_sem=k)

        VEC = Buf("vec")
        for l in range(2):
            small_load(ngT[:, l, :], ng_d[l].rearrange("(kc p) -> p kc", p=128), [VEC])
            small_load(blb[:, l, :], blb_d[l].rearrange("(fc p) -> p fc", p=128), [VEC])
        small_load(lng[:], alng_d[0].rearrange("(fc p) -> p fc", p=128), [VEC])
        small_load(lnb[:], alnb_d[0].rearrange("(fc p) -> p fc", p=128), [VEC])
        small_load(gng[:], bgn_d[0].rearrange("(p o) -> p o", o=1), [VEC])
        small_load(bs_bc, abs_d[0].partition_broadcast(128), [TB[4], TB[5]])
        small_load(fgain_bc[:], fg_d.partition_broadcast(128), [FGB])
        for s in range(NSEQ):
            small_load(cT[:, :, s], c_d[s].rearrange("(kc p) -> p kc", p=128), [VEC])
        for l in range(2):
            small_load(bT[:, l, :], bada_d[l].rearrange("(c p) -> p c", p=128), [VEC])

        A("vector", lambda e: e.tensor_scalar(out=lngh[:], in0=lng[:], scalar1=0.5, scalar2=None, op0=ALU.mult),
          reads=[VEC], writes=[CONST])
        A("vector", lambda e: e.tensor_scalar(out=lnbh[:], in0=lnb[:], scalar1=0.5, scalar2=None, op0=ALU.mult),
          reads=[VEC], writes=[CONST])
        A("vector", lambda e: e.tensor_scalar(out=gngh[:], in0=gng[:], scalar1=0.5, scalar2=None, op0=ALU.mult),
          reads=[VEC], writes=[CONST])
        A("vector", lambda e: e.tensor_scalar(out=bs_bc, in0=bs_bc, scalar1=0.5, scalar2=None, op0=ALU.mult),
          reads=[TB[4], TB[5]], writes=[TB[4], TB[5]])
        LBB = Buf("lb")
        A("vector", lambda e: e.tensor_tensor(out=lbt[:, 0, :], in0=blb[:, 0, :], in1=blb[:, 1, :], op=ALU.subtract),
          reads=[VEC], writes=[LBB])
        A("scalar", lambda e: e.activation(out=lbt[:, 1, :], in_=lbt[:, 0, :], func=AF.Exp), reads=[LBB], writes=[LBB])
        A("vector", lambda e: e.tensor_scalar(out=lbt[:, 2, :], in0=lbt[:, 1, :], scalar1=1.0, scalar2=None, op0=ALU.add),
          reads=[LBB], writes=[LBB])
        A("vector", lambda e: e.reciprocal(out=lbt[:, 3, :], in_=lbt[:, 2, :]), reads=[LBB], writes=[LBB])
        A("vector", lambda e: e.tensor_scalar(out=c1[:], in0=lbt[:, 3, :], scalar1=-0.5, scalar2=0.5, op0=ALU.mult, op1=ALU.add),
          reads=[LBB], writes=[CONST])
        A("vector", lambda e: e.tensor_scalar(out=c2[:], in0=lbt[:, 3, :], scalar1=0.5, scalar2=0.5, op0=ALU.mult, op1=ALU.add),
          reads=[LBB], writes=[CONST])
        A("vector", lambda e: e.tensor_scalar(out=nc1[:], in0=lbt[:, 3, :], scalar1=0.5, scalar2=-0.5, op0=ALU.mult, op1=ALU.add),
          reads=[LBB], writes=[CONST])
        A("scalar", lambda e: e.activation(out=csig[:], in_=cT[:], func=AF.Tanh, scale=0.5), reads=[VEC], writes=[LBB])
        A("vector", lambda e: e.tensor_scalar(out=csig[:], in0=csig[:], scalar1=0.5, scalar2=0.5, op0=ALU.mult, op1=ALU.add),
          reads=[LBB], writes=[LBB])
        A("vector", lambda e: e.tensor_tensor(out=cact[:], in0=csig[:], in1=cT[:], op=ALU.mult), reads=[LBB, VEC], writes=[CONST])

        wsl = tpool[:, 0:2, :].rearrange("p a (g s) -> p (a g) s", s=128)
        wsTf = tpool[:, 2:4, :].rearrange("p a (g s) -> p (a g) s", s=128)
        k = dkey("c")
        A("sync", lambda e: e.dma_start(out=wsl, in_=aws_d[0].rearrange("g t s -> t g s")), writes=[TB[0], TB[1]], dma_sem=k)
        for half in range(2):
            ps, pb = nextps()
            for j in range(4):
                g = half * 4 + j
                A("tensor", lambda e, ps=ps, g=g, j=j: e.transpose(out=ps[:, j * 128:(j + 1) * 128], in_=wsl[:, g, :], identity=ident[:]),
                  reads=[TB[0], TB[1], CONST], writes=[pb])
            A("vector", lambda e, ps=ps, half=half: e.tensor_copy(out=tpool[:, 2 + half, :], in_=ps[:]), reads=[pb], writes=[TB[2 + half]])
        A("gpsimd", lambda e: e.memset(wsTf[64:128, :, 0:64], 0.0), reads=[TB[2], TB[3]], writes=[TB[2], TB[3]])
        A("vector", lambda e: e.tensor_copy(out=wsT_bf[:], in_=wsTf), reads=[TB[2], TB[3]], writes=[CONST])
        for half in range(2):
            ps, pb = nextps()
            A("tensor", lambda e, ps=ps, half=half: e.matmul(ps[:], lhsT=ones_f[:], rhs=tpool[:, 2 + half, :], start=True, stop=True),
              reads=[TB[2 + half], CONST], writes=[pb])
            for j in range(4):
                g = half * 4 + j
                for fc in (2 * g, 2 * g + 1):
                    A("vector", lambda e, ps=ps, j=j, g=g, fc=fc: e.scalar_tensor_tensor(
                        out=Ch[:, fc, :], in0=ps[:, j * 128:(j + 1) * 128], scalar=lnbh[:, fc:fc + 1], in1=bsh[:, g, :],
                        op0=ALU.mult, op1=ALU.add), reads=[pb, CONST, VEC, TB[4], TB[5]], writes=[CONST])

        PF = [TB[4 * i:4 * i + 4] for i in range(3)]
        pf_ap = [tpool[:, 4 * i:4 * i + 4, :].rearrange("p a (k c) -> p (a k) c", c=256) for i in range(3)]
        pb16 = [vhat[:, i, :].rearrange("p (k c) -> p k c", c=256) for i in range(3)]
        stage_ctr = [0]

        def stage_load(W2d, r0, c0):
            i = stage_ctr[0] % 3
            stage_ctr[0] += 1
            src = W2d[r0:r0 + 1024, c0:c0 + 256].rearrange("(kc p) c -> p kc c", p=128)
            A("sync", lambda e: e.dma_start(out=pf_ap[i], in_=src), writes=PF[i], dma_sem="pf%d" % i)
            return i

        MODB = Buf("modr")
        for l in range(2):
            psT, pbT = psum[7], PB[7]
            for cb in range(12):
                i = stage_load(wada_d[l], 0, cb * 256)
                ps, pb = nextps()
                for kc in range(KC):
                    A("tensor", lambda e, ps=ps, i=i, kc=kc: e.matmul(ps[0:NSEQ, 0:256], lhsT=cact[:, kc, :], rhs=pf_ap[i][:, kc, :],
                                                                      start=(kc == 0), stop=(kc == KC - 1)),
                      reads=PF[i] + [CONST], writes=[pb])
                r = cb % 2
                A("vector", lambda e, ps=ps, r=r: e.tensor_copy(out=rowt[:, r, :], in_=ps[0:NSEQ, 0:256]), reads=[pb], writes=[ROWB[r]])
                for j in range(2):
                    col = (cb * 2 + j) * NSEQ
                    A("tensor", lambda e, psT=psT, r=r, j=j, col=col: e.transpose(out=psT[:, col:col + NSEQ], in_=rowt[:, r, j * 128:(j + 1) * 128],
                                                                                identity=ident[0:NSEQ, 0:NSEQ]),
                      reads=[ROWB[r], CONST], writes=[pbT])
            A("vector", lambda e, psT=psT, l=l: e.tensor_tensor(out=modT[:, l, :, :], in0=psT[:, 0:24 * NSEQ].rearrange("p (c s) -> p c s", s=NSEQ),
                                                                in1=bT[:, l, :].unsqueeze(2).broadcast_to([128, 24, NSEQ]), op=ALU.add),
              reads=[pbT, VEC], writes=[MODB])
        for l in range(2):
            for s in range(NSEQ):
                A("vector", lambda e, l=l, s=s: e.scalar_tensor_tensor(out=gsT[:, l, :, s], in0=modT[:, l, 8:16, s], scalar=1.0,
                                                                       in1=ngT[:, l, :], op0=ALU.add, op1=ALU.mult),
                  reads=[MODB, VEC], writes=[CONST])

        w_ctr = [0]
        first_pass = [True]
        converted = set()

        def wload(blk):
            i = w_ctr[0] % NW
            w_ctr[0] += 1
            if first_pass[0] and blk not in converted:
                converted.add(blk)
                for hf in range(2):
                    W2d, r0, c0 = blocks[blk][hf]
                    src = W2d[r0:r0 + 1024, c0:c0 + 256].rearrange("(kc p) c -> p kc c", p=128)
                    A("gpsimd", lambda e, src=src, hf=hf: e.dma_start(out=wring[:, i, :, hf * 256:(hf + 1) * 256], in_=src),
                      writes=[WBh[i][hf]], dma_sem="wc%d_%d" % (i, hf))
                A("sync", lambda e: e.dma_start(out=wsc[blk], in_=wring[:, i, :, :]), reads=WBh[i], writes=[WSC[blk]], dma_sem="wst%d" % i)
            else:
                A("sync", lambda e: e.dma_start(out=wring[:, i, :, :], in_=wsc[blk]), reads=[WSC[blk]], writes=WBh[i], dma_sem="w%d" % i)
            return wring[:, i, :, :], WBh[i]

        def hprep(l, s, par):
            for stl in range(NST):
                hprep_pre(l, s, par, stl)
                hprep_rest(l, s, par, stl)

        def hprep_pre(l, s, par, stl):
            if True:
                xn_i = 12 + 2 * stl
                xn = T2(xn_i)
                xnb = [TB[xn_i], TB[xn_i + 1]]
                A("scalar", lambda e, stl=stl, xn=xn: e.activation(out=xn, in_=xt[:, stl, :], func=AF.Square,
                                                                   accum_out=ss[:, par, stl:stl + 1]),
                  reads=[XB[stl]], writes=xnb + [SSB[par][stl]])
                A("gpsimd", lambda e, stl=stl: e.tensor_scalar(out=msq[:, par, stl:stl + 1], in0=ss[:, par, stl:stl + 1],
                                                               scalar1=1.0 / D, scalar2=EPS, op0=ALU.mult, op1=ALU.add),
                  reads=[SSB[par][stl]], writes=[SSB[par][stl]])
                A("gpsimd", lambda e, stl=stl: e.tensor_tensor(out=rstd[:, par, stl:stl + 1], in0=msq[:, par, stl:stl + 1],
                                                               in1=mhalf[:, 0:1], op=ALU.pow),
                  reads=[SSB[par][stl], CONST], writes=[SSB[par][stl]])
                A("vector", lambda e, stl=stl, xn=xn: e.tensor_scalar(out=xn, in0=xt[:, stl, :], scalar1=rstd[:, par, stl:stl + 1],
                                                                      scalar2=None, op0=ALU.mult),
                  reads=[XB[stl], SSB[par][stl]], writes=xnb)

        def hprep_rest(l, s, par, stl):
            if True:
                xn_i = 12 + 2 * stl
                xn = T2(xn_i)
                xnb = [TB[xn_i], TB[xn_i + 1]]
                for half in range(2):
                    ps, pb = nextps()
                    for j in range(4):
                        kc = half * 4 + j
                        A("tensor", lambda e, ps=ps, j=j, kc=kc, xn=xn: e.transpose(out=ps[:, j * 128:(j + 1) * 128],
                                                                                    in_=xn[:, kc * 128:(kc + 1) * 128], identity=ident[:]),
                          reads=xnb + [CONST], writes=[pb])
                    for j in range(4):
                        kc = half * 4 + j
                        if j % 2 == 0:
                            A("vector", lambda e, ps=ps, j=j, kc=kc, stl=stl: e.tensor_scalar(
                                out=hT[:, kc, stl * 128:(stl + 1) * 128], in0=ps[:, j * 128:(j + 1) * 128],
                                scalar1=gsT[:, l, kc, s:s + 1], scalar2=modT[:, l, kc, s:s + 1], op0=ALU.mult, op1=ALU.add),
                              reads=[pb, CONST, MODB], writes=[HB[stl]])
                        else:
                            A("scalar", lambda e, ps=ps, j=j, kc=kc, stl=stl: e.activation(
                                out=hT[:, kc, stl * 128:(stl + 1) * 128], in_=ps[:, j * 128:(j + 1) * 128], func=AF.Identity,
                                scale=gsT[:, l, kc, s:s + 1], bias=modT[:, l, kc, s:s + 1]),
                              reads=[pb, CONST, MODB], writes=[HB[stl]])

        def outproj_evac(l, ps, pb, ch, stl):
            ti = 4 + (stl % 2)
            A("vector", lambda e: e.tensor_tensor(out=T(ti), in0=ps[:], in1=gate_bc[:, l, ch * 512:(ch + 1) * 512], op=ALU.mult),
              reads=[pb, GTB[l]], writes=[TB[ti]])
            A("gpsimd", lambda e: e.tensor_tensor(out=xt[:, stl, ch * 512:(ch + 1) * 512], in0=xt[:, stl, ch * 512:(ch + 1) * 512], in1=T(ti), op=ALU.add),
              reads=[TB[ti], XB[stl]], writes=[XB[stl]])

        def outproj_ch0(l, OB):
            pss = [nextps() for _ in range(NST)]
            for kh in range(2):
                wb, wbuf = wload(OB[(kh, 0)])
                for stl in range(NST):
                    ps, pb = pss[stl]
                    for kc in range(KC):
                        fc = kh * 8 + kc
                        A("tensor", lambda e, ps=ps, wb=wb, kc=kc, fc=fc, stl=stl, kh=kh: e.matmul(
                            ps[:], lhsT=yT[:, fc, stl * 128:(stl + 1) * 128], rhs=wb[:, kc, :],
                            start=(kh == 0 and kc == 0), stop=(kh == 1 and kc == KC - 1)),
                          reads=[YB[fc]] + wbuf, writes=[pb])
            w1 = [wload(OB[(kh, 1)]) for kh in range(2)]
            for stl in range(NST):
                ps, pb = pss[stl]
                outproj_evac(l, ps, pb, 0, stl)
            return w1

        def outproj_ch1_st(l, w1, stl):
            ps, pb = nextps()
            for kh in range(2):
                wb, wbuf = w1[kh]
                for kc in range(KC):
                    fc = kh * 8 + kc
                    A("tensor", lambda e, ps=ps, wb=wb, kc=kc, fc=fc, kh=kh: e.matmul(
                        ps[:], lhsT=yT[:, fc, stl * 128:(stl + 1) * 128], rhs=wb[:, kc, :],
                        start=(kh == 0 and kc == 0), stop=(kh == 1 and kc == KC - 1)),
                      reads=[YB[fc]] + wbuf, writes=[pb])
            outproj_evac(l, ps, pb, 1, stl)

        def boundary(l, OB, fin_st, pre_st, rest_st):
            w1 = outproj_ch0(l, OB)
            a_ = [record(outproj_ch1_st, l, w1, stl)[0] for stl in range(NST)]
            if fin_st is None:
                pr = [record(pre_st, stl)[0] for stl in range(NST)]
                rs = [record(rest_st, stl)[0] for stl in range(NST)]
                emit(a_[0])
                emit(merge([a_[1], pr[0]]))
                emit(merge([a_[2], pr[1]]))
                emit(merge([a_[3], pr[2]]))
                emit(pr[3])
                emit(merge(rs))
            else:
                fn = [record(fin_st, stl)[0] for stl in range(NST)]
                ch = [record(pre_st, stl)[0] + record(rest_st, stl)[0] for stl in range(NST)]
                emit(a_[0])
                emit(merge([a_[1], fn[0]]))
                emit(merge([a_[2], fn[1]]))
                emit(merge([a_[3], fn[2]]))
                emit(fn[3])
                emit(merge(ch))

        def layer0(s, par):
            fence()
            dump("hT", hT[:], HB, BF16)
            dump("gsT", gsT[:], [CONST])
            dump("modT", modT[:], [MODB])
            dump("rstd", rstd[:], [b for r in SSB for b in r])
            for half in range(2):
                for vb in range(4):
                    wb, wbuf = wload(L0_V[vb])
                    for q in range(2):
                        stl = half * 2 + q
                        vbig_i = 6 + 4 * q
                        ps, pb = nextps()
                        for kc in range(KC):
                            A("tensor", lambda e, ps=ps, wb=wb, kc=kc, stl=stl: e.matmul(
                                ps[:], lhsT=hT[:, kc, stl * 128:(stl + 1) * 128], rhs=wb[:, kc, :], start=(kc == 0), stop=(kc == KC - 1)),
                              reads=[HB[stl]] + wbuf, writes=[pb])
                        A("scalar", lambda e, ps=ps, vbig_i=vbig_i, vb=vb: e.activation(out=T(vbig_i + vb), in_=ps[:], func=AF.Gelu_apprx_tanh),
                          reads=[pb], writes=[TB[vbig_i + vb]])
                        A("vector", lambda e, vbig_i=vbig_i, vb=vb, q=q: e.bn_stats(out=bnst[:, q, vb, :], in_=T(vbig_i + vb)),
                          reads=[TB[vbig_i + vb]], writes=[BNB[q]])
                for q in range(2):
                    stl = half * 2 + q
                    vbig_i = 6 + 4 * q
                    A("vector", lambda e, q=q: e.bn_aggr(out=mv[:, q, :], in_=bnst[:, q, :, :].rearrange("p a b -> p (a b)")),
                      reads=[BNB[q]], writes=[BNB[q]])
                    A("gpsimd", lambda e, q=q: e.tensor_scalar(out=vtmp[:, q, :], in0=mv[:, q, 1:2], scalar1=EPS, scalar2=None, op0=ALU.add),
                      reads=[BNB[q]], writes=[BNB[q]])
                    A("gpsimd", lambda e, q=q: e.tensor_tensor(out=vrs[:, q, :], in0=vtmp[:, q, :], in1=mhalf[:, 0:1], op=ALU.pow),
                      reads=[BNB[q], CONST], writes=[BNB[q]])
                    A("vector", lambda e, q=q, stl=stl, vbig_i=vbig_i: e.tensor_scalar(
                        out=vhat[:, stl, :], in0=T4(vbig_i), scalar1=mv[:, q, 0:1], scalar2=vrs[:, q, :], op0=ALU.subtract, op1=ALU.mult),
                      reads=TB[vbig_i:vbig_i + 4] + [BNB[q]], writes=[VHB[stl]])
            dump("vhat", vhat[:], VHB, BF16)
            dump("mv", mv[:], BNB)
            dump("vbig", T4(10), TB[10:14])
            for fg in range(4):
                ub, ubuf = wload(L0_U[fg])
                gb, gbuf = wload(L0_G[fg])
                for j in range(4):
                    fc = fg * 4 + j
                    g = fc // 2
                    r = fc % 2
                    iu, itg, isg, is_, iy = 14 + r, 16 + r, 18 + r, 20 + r, 22 + r
                    psu, pbu = nextps()
                    for kc in range(KC):
                        A("tensor", lambda e, psu=psu, ub=ub, kc=kc, j=j: e.matmul(psu[:], lhsT=ub[:, kc, j * 128:(j + 1) * 128], rhs=hT[:, kc, :],
                                                                                 start=(kc == 0), stop=(kc == KC - 1)),
                          reads=HB + ubuf, writes=[pbu])
                    A("scalar", lambda e, psu=psu, iu=iu: e.activation(out=T(iu), in_=psu[:], func=AF.Gelu_apprx_tanh), reads=[pbu], writes=[TB[iu]])
                    psg, pbg = nextps()
                    for kc in range(KC):
                        A("tensor", lambda e, psg=psg, gb=gb, kc=kc, j=j: e.matmul(psg[:], lhsT=gb[:, kc, j * 128:(j + 1) * 128], rhs=hT[:, kc, :],
                                                                                 start=(kc == 0), stop=(kc == KC - 1)),
                          reads=HB + gbuf, writes=[pbg])
                    A("scalar", lambda e, psg=psg, itg=itg: e.activation(out=T(itg), in_=psg[:], func=AF.Tanh, scale=0.5), reads=[pbg], writes=[TB[itg]])
                    A("vector", lambda e, psg=psg, itg=itg, isg=isg: e.scalar_tensor_tensor(out=T(isg), in0=T(itg), scalar=1.0, in1=psg[:],
                                                                                            op0=ALU.add, op1=ALU.mult),
                      reads=[pbg, TB[itg]], writes=[TB[isg]])
                    pss, pbs = nextps()
                    for stl in range(NST):
                        A("tensor", lambda e, pss=pss, stl=stl, fc=fc, g=g: e.matmul(
                            pss[:, stl * 128:(stl + 1) * 128], lhsT=vhat[:, stl, fc * 128:(fc + 1) * 128], rhs=wsT_bf[:, g, :], start=True, stop=True),
                          reads=[VHB[stl], CONST], writes=[pbs])
                    A("vector", lambda e, pss=pss, is_=is_, fc=fc: e.scalar_tensor_tensor(
                        out=T(is_).rearrange("p (a b) -> p a b", b=128), in0=pss[:].rearrange("p (a b) -> p a b", b=128),
                        scalar=lngh[:, fc:fc + 1], in1=Ch[:, fc:fc + 1, :].broadcast_to([128, NST, 128]), op0=ALU.mult, op1=ALU.add),
                      reads=[pbs, CONST], writes=[TB[is_]])
                    A("gpsimd", lambda e, is_=is_, iu=iu, iy=iy: e.tensor_tensor(out=T(iy), in0=T(is_), in1=T(iu), op=ALU.mult),
                      reads=[TB[is_], TB[iu]], writes=[TB[iy]])
                    A("vector", lambda e, iy=iy, isg=isg, fc=fc: e.tensor_tensor(out=yT[:, fc, :], in0=T(iy), in1=T(isg), op=ALU.mult),
                      reads=[TB[iy], TB[isg]], writes=[YB[fc]])
                    if fc == 0:
                        dump("u0", T(iu), [TB[iu]])
                        dump("sg0", T(isg), [TB[isg]])
                        dump("s0", T(is_), [TB[is_]])
            dump("yT", yT[:], YB, BF16)
            dump("gate_bc", gate_bc[:], GTB)
            dump("Ch", Ch[:], [CONST])

        LN_HALF = math.log(0.5)

        def layer1(s, par):
            fence()

            def tq_(i, hh):
                return 6 + 6 * (i % 3) + hh

            def tf_(i, hh):
                return 8 + 6 * (i % 3) + hh

            def tg_(i, hh):
                return 10 + 6 * (i % 3) + hh

            def X(hp):
                p3 = hp % 3
                ab, abuf = wload(L1_A[hp])
                bb, bbuf = wload(L1_B[hp])
                for hh in range(2):
                    for (blkap, blkbuf, c0, slot, stt) in ((ab, abuf, hh * 128, tq_(hp, hh), True),
                                                          (ab, abuf, 256 + hh * 128, tf_(hp, hh), False),
                                                          (bb, bbuf, hh * 128, tg_(hp, hh), True)):
                        ps, pb = nextps("X")
                        for kc in range(KC):
                            A("tensor", lambda e, ps=ps, blkap=blkap, kc=kc, c0=c0: e.matmul(ps[:], lhsT=blkap[:, kc, c0:c0 + 128], rhs=hT[:, kc, :],
                                                                                         start=(kc == 0), stop=(kc == KC - 1)),
                              reads=HB + blkbuf, writes=[pb])
                        A("scalar", lambda e, ps=ps, slot=slot: e.activation(out=T(slot), in_=ps[:], func=AF.Tanh, scale=0.5), reads=[pb], writes=[TB[slot]])
                        if stt:
                            A("vector", lambda e, ps=ps, slot=slot: e.scalar_tensor_tensor(out=T(slot), in0=T(slot), scalar=1.0, in1=ps[:],
                                                                                           op0=ALU.add, op1=ALU.mult),
                              reads=[pb, TB[slot]], writes=[TB[slot]])
                for stl in range(NST):
                    ps, pb = nextps("X")
                    for kc in range(KC):
                        A("tensor", lambda e, ps=ps, bb=bb, kc=kc, stl=stl: e.matmul(
                            ps[:, 0:256], lhsT=hT[:, kc, stl * 128:(stl + 1) * 128], rhs=bb[:, kc, 256:512], start=(kc == 0), stop=(kc == KC - 1)),
                          reads=[HB[stl]] + bbuf, writes=[pb])
                    A("scalar", lambda e, ps=ps, stl=stl, p3=p3: e.activation(out=v_tok[:, p3, stl, :], in_=ps[:, 0:256], func=AF.Copy),
                      reads=[pb], writes=[VTB[p3][stl]])

            def YL(hp):
                for hh in range(2):
                    h = 2 * hp + hh
                    itf = tf_(hp, hh)
                    A("scalar", lambda e, hh=hh, h=h, itf=itf: e.activation(out=T(24 + hh), in_=T(itf), func=AF.Ln, scale=c1[:, h:h + 1], bias=c2[:, h:h + 1]),
                      reads=[TB[itf], CONST], writes=[TB[24 + hh]])
                    A("gpsimd", lambda e, hh=hh, h=h, itf=itf: e.tensor_scalar(out=T(itf), in0=T(itf), scalar1=nc1[:, h:h + 1], scalar2=c1[:, h:h + 1],
                                                                               op0=ALU.mult, op1=ALU.add),
                      reads=[TB[itf], CONST], writes=[TB[itf]])

            def Pfx(hp, hh):
                p2 = hp % 2
                iL, iA = 24 + hh, 26 + hh
                iq, ik = tq_(hp, hh), tf_(hp, hh)
                cl = CLB[p2][hh]
                A("vector", lambda e: e.tensor_tensor_scan(out=T(iA), data0=mask01[:], data1=T(iL), initial=0.0, op0=ALU.mult, op1=ALU.add),
                  reads=[TB[iL], CONST], writes=[TB[iA]])
                a3 = T(iA).rearrange("p (c t) -> p c t", t=128)
                A("gpsimd", lambda e: e.tensor_tensor(out=T(iL).rearrange("p (c t) -> p c t", t=128), in0=a3,
                                                      in1=a3[:, :, 63:64].broadcast_to([128, NST, 128]), op=ALU.subtract),
                  reads=[TB[iA]], writes=[TB[iL]])
                A("vector", lambda e: e.tensor_tensor(out=cols[:, p2, hh, 0, :], in0=a3[:, :, 127], in1=a3[:, :, 63], op=ALU.subtract),
                  reads=[TB[iA]], writes=[cl])
                A("scalar", lambda e: e.activation(out=cols[:, p2, hh, 2, :], in_=a3[:, :, 127], func=AF.Exp), reads=[TB[iA], cl], writes=[cl])
                A("scalar", lambda e: e.activation(out=cols[:, p2, hh, 3, :], in_=a3[:, :, 63], func=AF.Exp), reads=[TB[iA], cl], writes=[cl])
                A("scalar", lambda e: e.activation(out=cols[:, p2, hh, 4, :], in_=cols[:, p2, hh, 0, :], func=AF.Exp), reads=[cl], writes=[cl])
                A("scalar", lambda e: e.activation(out=T(iA), in_=T(iL), func=AF.Exp, bias=LN_HALF), reads=[TB[iL], TB[iA]], writes=[TB[iA]])
                A("scalar", lambda e: e.activation(out=T(iL), in_=T(iL), func=AF.Exp, scale=-1.0), reads=[TB[iL]], writes=[TB[iL]])
                A("gpsimd", lambda e: e.tensor_tensor(out=q_inT[:, p2, hh, :], in0=T(iq), in1=T(iA), op=ALU.mult),
                  reads=[TB[iq], TB[iA]], writes=[QIB[p2][hh]])
                A("vector", lambda e: e.tensor_tensor(out=k_inT[:, p2, hh, :], in0=T(ik), in1=T(iL), op=ALU.mult),
                  reads=[TB[ik], TB[iL]], writes=[KIB[p2][hh]])

            def Mid(hp, hh):
                p2 = hp % 2
                p3 = hp % 3
                h = 2 * hp + hh
                cl = CLB[p2][hh]
                qib, kib = QIB[p2][hh], KIB[p2][hh]
                bankA = (psum[4 + 2 * hh], PB[4 + 2 * hh])
                bankB = (psum[5 + 2 * hh], PB[5 + 2 * hh])
                ps, pb = bankA
                psb = ps[:].bitcast(BF16)
                pssc, pbsc = bankB
                psU, pbU = bankA
                pso, pbo = bankB
                psm, pbm = bankA
                for stl in range(NST):
                    A("tensor", lambda e, psb=psb, stl=stl: e.transpose(out=psb[:, stl * 128:(stl + 1) * 128], in_=k_inT[:, p2, hh, stl * 128:(stl + 1) * 128],
                                                                     identity=ident_bf[:]), reads=[kib, CONST], writes=[pb])
                for stl in range(NST):
                    A("tensor", lambda e, pssc=pssc, stl=stl: e.matmul(pssc[:, stl * 128:(stl + 1) * 128], lhsT=k_inT[:, p2, hh, stl * 128:(stl + 1) * 128],
                                                                     rhs=q_inT[:, p2, hh, stl * 128:(stl + 1) * 128], start=True, stop=True),
                      reads=[kib, qib], writes=[pbsc])
                A("scalar", lambda e, psb=psb: e.activation(out=k_tok[:, hh, :, :].rearrange("p b c -> p (b c)"), in_=psb[:, 0:NST * 128], func=AF.Copy),
                  reads=[pb], writes=[KTB[hh]])
                A("vector", lambda e, pssc=pssc: e.tensor_tensor(out=scm[:, hh, :, :], in0=pssc[:].rearrange("p (a b) -> p a b", b=128),
                                                                 in1=cmask[:].unsqueeze(1).broadcast_to([128, NST, 128]), op=ALU.mult),
                  reads=[pbsc, CONST], writes=[SCB[hh]])
                A("gpsimd", lambda e: e.tensor_scalar(out=stq[:, hh, 0, :], in0=state[:, h, :], scalar1=cols[:, p2, hh, 3, 0:1], scalar2=1.0, op0=ALU.mult, op1=ALU.mult),
                  reads=[STB[h], cl], writes=[SQTB[hh]])
                for stl in range(NST):
                    A("tensor", lambda e, psU=psU, stl=stl: e.matmul(psU[:, stl * 128:(stl + 1) * 128], lhsT=k_tok[:, hh, stl, :],
                                                                   rhs=v_tok[:, p3, stl, hh * 128:(hh + 1) * 128], start=True, stop=True),
                      reads=[KTB[hh], VTB[p3][stl]], writes=[pbU])
                A("vector", lambda e, psU=psU: e.tensor_tensor(out=utmp[:, hh, :, :], in0=psU[:].rearrange("p (a b) -> p a b", b=128),
                                                               in1=cols[:, p2, hh, 4, :].unsqueeze(2).broadcast_to([128, NST, 128]), op=ALU.mult),
                  reads=[pbU, cl], writes=[UTB[hh]])
                for stl in range(NST):
                    src = state[:, h, :] if stl == 0 else Sh[:, hh, stl - 1, :]
                    dst = state[:, h, :] if stl == NST - 1 else Sh[:, hh, stl, :]
                    rd = [UTB[hh], cl] + ([STB[h]] if stl == 0 else [SHB[hh]])
                    wr = [STB[h]] if stl == NST - 1 else [SHB[hh]]
                    A("vector", lambda e, src=src, dst=dst, stl=stl: e.scalar_tensor_tensor(
                        out=dst, in0=src, scalar=cols[:, p2, hh, 2, stl:stl + 1], in1=utmp[:, hh, stl, :], op0=ALU.mult, op1=ALU.add),
                      reads=rd, writes=wr)
                A("gpsimd", lambda e: e.tensor_tensor(out=stq[:, hh, 1:NST, :], in0=Sh[:, hh, :, :],
                                                      in1=cols[:, p2, hh, 3, 1:NST].unsqueeze(2).broadcast_to([128, NST - 1, 128]), op=ALU.mult),
                  reads=[SHB[hh], cl, SQTB[hh]], writes=[SQTB[hh]])
                for stl in range(NST):
                    A("tensor", lambda e, pso=pso, stl=stl: e.matmul(pso[:, stl * 128:(stl + 1) * 128], lhsT=v_tok[:, p3, stl, hh * 128:(hh + 1) * 128],
                                                                   rhs=scm[:, hh, stl, :], start=True, stop=False),
                      reads=[VTB[p3][stl], SCB[hh]], writes=[pbo])
                    A("tensor", lambda e, pso=pso, stl=stl: e.matmul(pso[:, stl * 128:(stl + 1) * 128], lhsT=stq[:, hh, stl, :],
                                                                   rhs=q_inT[:, p2, hh, stl * 128:(stl + 1) * 128], start=False, stop=True),
                      reads=[SQTB[hh], qib], writes=[pbo])
                A("scalar", lambda e, pso=pso: e.activation(out=sqb[:, hh, :], in_=pso[:], func=AF.Square), reads=[pbo], writes=[SQB[hh]])
                A("tensor", lambda e, psm=psm: e.matmul(psm[:], lhsT=onesm_bf[:], rhs=sqb[:, hh, :], start=True, stop=True),
                  reads=[SQB[hh], CONST], writes=[pbm])

            def TailLn(hp, hh):
                psm, pbm = psum[4 + 2 * hh], PB[4 + 2 * hh]
                A("scalar", lambda e: e.activation(out=T(28 + hh), in_=psm[:], func=AF.Ln, bias=EPS), reads=[pbm], writes=[TB[28 + hh]])

            def TailRest(hp, hh):
                h = 2 * hp + hh
                pso, pbo = psum[5 + 2 * hh], PB[5 + 2 * hh]
                irs, ion, ig = 28 + hh, 30 + hh, tg_(hp, hh)
                A("scalar", lambda e: e.activation(out=T(irs), in_=T(irs), func=AF.Exp, scale=-0.5), reads=[TB[irs]], writes=[TB[irs]])
                A("vector", lambda e: e.tensor_tensor(out=T(ion), in0=pso[:], in1=T(irs), op=ALU.mult), reads=[pbo, TB[irs]], writes=[TB[ion]])
                A("vector", lambda e: e.scalar_tensor_tensor(out=yT[:, h, :], in0=T(ion), scalar=gngh[:, 0:1], in1=T(ig), op0=ALU.mult, op1=ALU.mult),
                  reads=[TB[ion], TB[ig], CONST], writes=[YB[h]])

            def both(f, hp):
                return merge([record(f, hp, 0)[0], record(f, hp, 1)[0]])

            for i in range(-2, 8):
                lists = []
                if i + 2 < 8:
                    lists.append(record(X, i + 2)[0])
                if 0 <= i + 1 < 8:
                    lists.append(both(Pfx, i + 1))
                if i >= 0:
                    lists.append(both(Mid, i))
                emit(merge(lists))
                if i >= 0:
                    emit(both(TailLn, i))
                if i + 2 < 8:
                    YL(i + 2)
                if i >= 0:
                    emit(both(TailRest, i))
            dump("yT1", yT[:], YB, BF16)

        def gate_setup(s):
            for l in range(2):
                for ch in range(2):
                    ps, pb = nextps()
                    for j in range(4):
                        kc = ch * 4 + j
                        A("vector", lambda e, l=l, kc=kc, j=j: e.tensor_scalar(out=T(4)[:, j * 128:(j + 1) * 128], in0=ident[:],
                                                                              scalar1=modT[:, l, 16 + kc, s:s + 1], scalar2=None, op0=ALU.mult),
                          reads=[MODB, CONST], writes=[TB[4]])
                    A("tensor", lambda e, ps=ps: e.matmul(ps[:], lhsT=ones_f[:], rhs=T(4), start=True, stop=True),
                      reads=[TB[4], CONST], writes=[pb])
                    A("vector", lambda e, ps=ps, l=l, ch=ch: e.tensor_copy(out=gate_bc[:, l, ch * 512:(ch + 1) * 512], in_=ps[:]), reads=[pb], writes=[GTB[l]])

        def xload_st(row0, stl, eng="scalar"):
            A(eng, lambda e: e.dma_start(out=xt[:, stl, :], in_=x_d[row0 + stl * 128: row0 + (stl + 1) * 128, :]),
              writes=[XB[stl]], dma_sem="xload%d" % stl)

        def final_st(row0, stl):
            oi = stl % 2
            ot = T2(6 + 2 * oi)
            otb = [TB[6 + 2 * oi], TB[7 + 2 * oi]]
            A("scalar", lambda e: e.activation(out=ot, in_=xt[:, stl, :], func=AF.Square, accum_out=ss[:, 2, stl:stl + 1]),
              reads=[XB[stl]], writes=otb + [SSB[2][stl]])
            A("gpsimd", lambda e: e.tensor_scalar(out=msq[:, 2, stl:stl + 1], in0=ss[:, 2, stl:stl + 1], scalar1=1.0 / D, scalar2=EPS,
                                                  op0=ALU.mult, op1=ALU.add), reads=[SSB[2][stl]], writes=[SSB[2][stl]])
            A("gpsimd", lambda e: e.tensor_tensor(out=rstd[:, 2, stl:stl + 1], in0=msq[:, 2, stl:stl + 1], in1=mhalf[:, 0:1], op=ALU.pow),
              reads=[SSB[2][stl], CONST], writes=[SSB[2][stl]])
            A("vector", lambda e: e.scalar_tensor_tensor(out=ot, in0=xt[:, stl, :], scalar=rstd[:, 2, stl:stl + 1], in1=fgain_bc[:],
                                                         op0=ALU.mult, op1=ALU.mult),
              reads=[XB[stl], SSB[2][stl], FGB], writes=otb)
            A("scalar", lambda e: e.dma_start(out=out_d[row0 + stl * 128: row0 + (stl + 1) * 128, :], in_=ot),
              reads=otb, writes=[OUTB[oi]], dma_sem="ost%d" % oi)

        tiles = [(s, j) for s in range(NSEQ) for j in range(NSUP)]
        for ti_, (s, j) in enumerate(tiles):
            row0 = s * SEQ + j * NT
            if ti_ == 0:
                for stl in range(NST):
                    xload_st(row0, stl, "sync")
                hprep(0, s, 0)
            if j == 0:
                gate_setup(s)
                if s > 0:
                    A("gpsimd", lambda e: e.memset(state[:], 0.0), reads=STB, writes=STB)
            layer0(s, 0)
            boundary(0, L0_O, None, lambda stl, s=s: hprep_pre(1, s, 1, stl), lambda stl, s=s: hprep_rest(1, s, 1, stl))
            if DEBUG:
                A("sync", lambda e, row0=row0: e.dma_start(out=dbg_d[row0:row0 + NT, :].rearrange("(a p) d -> p a d", p=128), in_=xt[:]),
                  reads=XB, writes=[DBGB], dma_sem="dbg")
            layer1(s, 1)
            if ti_ + 1 < len(tiles):
                s2, j2 = tiles[ti_ + 1]
                row1 = s2 * SEQ + j2 * NT

                def fin(stl, row0=row0, row1=row1):
                    final_st(row0, stl)
                    xload_st(row1, stl)

                def pre(stl, s2=s2):
                    hprep_pre(0, s2, 0, stl)

                def rest(stl, s2=s2):
                    hprep_rest(0, s2, 0, stl)
            else:
                def fin(stl, row0=row0):
                    final_st(row0, stl)

                def pre(stl):
                    pass

                def rest(stl):
                    pass
            boundary(1, L1_O, fin, pre, rest)
            first_pass[0] = False
            if ti_ == len(tiles) - 1:
                dump("fin_rstd", rstd[:], [b for r in SSB for b in r])
                dump("fin_ss", ss[:], [b for r in SSB for b in r])
                dump("fin_x", xt[:], XB)
                dump("fin_gate", gate_bc[:], GTB)
        A("sync", None, reads=OUTB + ([DBGB] if DEBUG else []), writes=OUTB)
        P.finalize(st)
    return nc


_CACHE = {}


def _get_nc(NSEQ, SEQ):
    key = (NSEQ, SEQ)
    if key not in _CACHE:
        _CACHE[key] = build(NSEQ, SEQ)
    return _CACHE[key]


def run_cores(inputs, n_cores, NSEQ, SEQ):
    f32 = lambda a: np.ascontiguousarray(np.asarray(a, dtype=np.float32))
    x = f32(inputs["x"])
    c = f32(inputs["c"])
    shared = {k: f32(inputs[k]) for k in ("norm_gain", "w_ada", "b_ada", "a_w_in", "a_ln_gain", "a_ln_bias", "a_w_s", "a_b_s",
                                          "a_w_out", "b_w_in", "b_lower_bounds", "b_gn_gain", "b_w_out", "final_gain")}
    in_maps = []
    for i in range(n_cores):
        m = dict(shared)
        m["x"] = np.ascontiguousarray(x[i * NSEQ:(i + 1) * NSEQ].reshape(NSEQ * SEQ, D))
        m["c"] = np.ascontiguousarray(c[i * NSEQ:(i + 1) * NSEQ])
        in_maps.append(m)
    nc = _get_nc(NSEQ, SEQ)
    res = run_bass_kernel_spmd(nc, in_maps, core_ids=list(range(n_cores)))
    outs = [np.asarray(r["out"]).reshape(NSEQ, SEQ, D) for r in res.results]
    if DEBUG:
        LAST.clear()
        LAST.append(res.results)
    return np.concatenate(outs, axis=0).astype(np.float32)


def kernel(**inputs):
    B, S, _ = inputs["x"].shape
    nseq = B // N_CORES
    return run_cores(inputs, N_CORES, nseq, S)
```
